# Optimizing a Trainium2 kernel written in Bass

```python
import jax, jax.numpy as jnp
from jax import lax
import numpy as np

D_MODEL = 1024
BATCH = 1
SEQ = 16384
DEPTH = 4
DEC_BATCH = 8
DEC_SEQ = 32
PAST_LEN = 2048

CHUNK = 64
PLE_DIM = 256
N_EVEN = (DEPTH + 1) // 2
N_ODD = DEPTH // 2
EPS = 1e-6
SSD_HEAD_DIM = 64
SSD_HEADS = D_MODEL // SSD_HEAD_DIM
SSD_WIDTH = SSD_HEADS * SSD_HEAD_DIM
SSD_GROUPS = 2
SSD_STATE = 128
CONV_WIDTH = 4
CONV_CH = SSD_WIDTH + 2 * SSD_GROUPS * SSD_STATE
POOL_WINDOWS = (2, 4, 8, 16)
POOL_GROUPS = len(POOL_WINDOWS)
POOL_WIDTH = D_MODEL
POOL_GROUP_DIM = POOL_WIDTH // POOL_GROUPS
POOL_HIST = max(POOL_WINDOWS) - 1
D_IN_EVEN = SSD_WIDTH + CONV_CH + SSD_HEADS + POOL_WIDTH
QK_NOPE = 64
QK_ROPE = 32
V_DIM = 64
MLA_HEADS = D_MODEL // V_DIM
Q_LORA = 256
KV_LORA = 256
D_IN_ODD = Q_LORA + KV_LORA + QK_ROPE
ROPE_THETA = 10000.0
ATTN_Q_BLOCK = 128
PEER_HEADS = 8
PEER_NKEYS = 128
PEER_EXPERTS = PEER_NKEYS * PEER_NKEYS
PEER_QDIM = 256
PEER_HALF = PEER_QDIM // 2
PEER_TOPK = 16
PEER_TOKEN_BLOCK = 128

kernel_name = "ssd_pool_mla_peer_streaming_step"


def rmsnorm(x, w):
    xf = x.astype(jnp.float32)
    y = xf * lax.rsqrt(jnp.mean(xf * xf, axis=-1, keepdims=True) + EPS)
    return (y * w.astype(jnp.float32)).astype(x.dtype)


def rope(x, pos):
    half = QK_ROPE // 2
    inv = ROPE_THETA ** (-jnp.arange(half, dtype=jnp.float32) / half)
    ang = pos.astype(jnp.float32)[:, None] * inv[None, :]
    if x.ndim == 4:
        ang = ang[:, None, :]
    cos, sin = jnp.cos(ang), jnp.sin(ang)
    xf = x.astype(jnp.float32)
    x1, x2 = xf[..., :half], xf[..., half:]
    return jnp.concatenate([x1 * cos - x2 * sin, x1 * sin + x2 * cos], axis=-1).astype(x.dtype)


def causal_conv(x_ext, w, b):
    c = x_ext.shape[-1]
    out = lax.conv_general_dilated(x_ext, w[:, None, :].astype(x_ext.dtype), window_strides=(1,),
                                   padding='VALID', dimension_numbers=('NWC', 'WIO', 'NWC'),
                                   feature_group_count=c)
    return out + b.astype(x_ext.dtype)


def ssd_scan(x, dt, a, bm, cm, h0, block):
    f32 = jnp.float32
    b, l, g, r, p = x.shape
    n = bm.shape[-1]
    c = l // block
    x = x.astype(f32).reshape(b, c, block, g, r, p)
    dt = dt.astype(f32).reshape(b, c, block, g, r)
    bm = bm.astype(f32).reshape(b, c, block, g, n)
    cm = cm.astype(f32).reshape(b, c, block, g, n)
    da_cs = jnp.cumsum(dt * a, axis=2)
    xdt = x * dt[..., None]
    causal = jnp.tril(jnp.ones((block, block), dtype=bool))
    diff = da_cs[:, :, :, None] - da_cs[:, :, None, :]
    decay = jnp.exp(jnp.where(causal[None, None, :, :, None, None], diff, -jnp.inf))
    cb = jnp.einsum('bclgn,bcsgn->bclsg', cm, bm)
    y_diag = jnp.einsum('bclsgr,bcsgrp->bclgrp', cb[..., None] * decay, xdt)
    decay_to_end = jnp.exp(da_cs[:, :, -1:] - da_cs)
    chunk_states = jnp.einsum('bcsgn,bcsgr,bcsgrp->bcgrpn', bm, decay_to_end, xdt)
    chunk_decay = jnp.exp(da_cs[:, :, -1])

    def step(hc, inp):
        s_c, d_c = inp
        return d_c[..., None, None] * hc + s_c, hc

    h_final, h_prev = lax.scan(step, h0.astype(f32),
                               (jnp.moveaxis(chunk_states, 1, 0), jnp.moveaxis(chunk_decay, 1, 0)))
    h_prev = jnp.moveaxis(h_prev, 0, 1)
    y_off = jnp.einsum('bclgn,bcgrpn,bclgr->bclgrp', cm, h_prev, jnp.exp(da_cs))
    return (y_diag + y_off).reshape(b, l, g, r, p), h_final


def even_mixer(hn, conv_st, ssm_st, pool_st, pos0, w_in, conv_w, conv_b, dt_bias, a_log, d_skip,
               ssd_norm_w, pool_w, pool_scale, w_out):
    f32 = jnp.float32
    bsz, l, _ = hn.shape
    r = SSD_HEADS // SSD_GROUPS
    proj = hn @ w_in
    z, xbc, dt, u = jnp.split(proj, [SSD_WIDTH, SSD_WIDTH + CONV_CH, SSD_WIDTH + CONV_CH + SSD_HEADS], axis=-1)
    xbc_ext = jnp.concatenate([conv_st.astype(xbc.dtype), xbc], axis=1)
    new_conv = xbc_ext[:, -(CONV_WIDTH - 1):]
    xbc = jax.nn.silu(causal_conv(xbc_ext, conv_w, conv_b))
    xs, bm, cm = jnp.split(xbc, [SSD_WIDTH, SSD_WIDTH + SSD_GROUPS * SSD_STATE], axis=-1)
    xs = xs.reshape(bsz, l, SSD_GROUPS, r, SSD_HEAD_DIM)
    bm = bm.reshape(bsz, l, SSD_GROUPS, SSD_STATE)
    cm = cm.reshape(bsz, l, SSD_GROUPS, SSD_STATE)
    dt = jax.nn.softplus(dt.astype(f32) + dt_bias.astype(f32)).reshape(bsz, l, SSD_GROUPS, r)
    a = -jnp.exp(a_log.astype(f32)).reshape(SSD_GROUPS, r)
    h0 = ssm_st.reshape(bsz, SSD_GROUPS, r, SSD_HEAD_DIM, SSD_STATE)
    block = CHUNK if l % CHUNK == 0 else l
    y, h_final = ssd_scan(xs, dt, a, bm, cm, h0, block)
    y = y + d_skip.astype(f32).reshape(SSD_GROUPS, r)[:, :, None] * xs.astype(f32)
    y = y.reshape(bsz, l, SSD_GROUPS, r * SSD_HEAD_DIM) * jax.nn.silu(z.astype(f32)).reshape(bsz, l, SSD_GROUPS, r * SSD_HEAD_DIM)
    y = y * lax.rsqrt(jnp.mean(y * y, axis=-1, keepdims=True) + EPS)
    y = (y.reshape(bsz, l, SSD_WIDTH) * ssd_norm_w.astype(f32)).astype(hn.dtype)
    new_ssm = h_final.reshape(bsz, SSD_HEADS, SSD_HEAD_DIM, SSD_STATE)
    u_ext = jnp.concatenate([pool_st.astype(u.dtype), u], axis=1)
    new_pool = u_ext[:, -POOL_HIST:]
    cs = jnp.pad(jnp.cumsum(u_ext.astype(f32), axis=1), ((0, 0), (1, 0), (0, 0)))
    end = cs[:, POOL_HIST + 1:]
    pos = pos0 + jnp.arange(l)
    means = []
    for gi, wsz in enumerate(POOL_WINDOWS):
        sl = slice(gi * POOL_GROUP_DIM, (gi + 1) * POOL_GROUP_DIM)
        start = cs[:, POOL_HIST + 1 - wsz:POOL_HIST + 1 - wsz + l, sl]
        cnt = jnp.minimum(pos + 1, wsz).astype(f32)[None, :, None]
        means.append((end[..., sl] - start) / cnt)
    pooled = (jnp.concatenate(means, axis=-1) - u.astype(f32)).reshape(bsz, l, POOL_GROUPS, POOL_GROUP_DIM)
    yp = jnp.einsum('blgc,gcd->blgd', pooled, pool_w.astype(f32)).reshape(bsz, l, POOL_WIDTH)
    yp = (yp * pool_scale.astype(f32)).astype(hn.dtype)
    out = jnp.concatenate([y, yp], axis=-1) @ w_out
    return out, new_conv, new_ssm, new_pool


def odd_mixer(hn, ckv_hist, kpe_hist, pos0, w_in, q_norm, kv_norm, w_uq, w_ukv, w_out):
    f32 = jnp.float32
    bsz, l, _ = hn.shape
    pos = pos0 + jnp.arange(l)
    proj = hn @ w_in
    cq, ckv, kpe = jnp.split(proj, [Q_LORA, Q_LORA + KV_LORA], axis=-1)
    q = (rmsnorm(cq, q_norm) @ w_uq).reshape(bsz, l, MLA_HEADS, QK_NOPE + QK_ROPE)
    q_nope, q_pe = q[..., :QK_NOPE], rope(q[..., QK_NOPE:], pos)
    ckv = rmsnorm(ckv, kv_norm)
    kpe = rope(kpe, pos)
    ckv_all = jnp.concatenate([ckv_hist.astype(ckv.dtype), ckv], axis=1)
    kpe_all = jnp.concatenate([kpe_hist.astype(kpe.dtype), kpe], axis=1)
    n_keys = ckv_all.shape[1]
    kv = (ckv_all @ w_ukv).reshape(bsz, n_keys, MLA_HEADS, QK_NOPE + V_DIM)
    k_nope, v = kv[..., :QK_NOPE], kv[..., QK_NOPE:]
    k_chunk = jnp.arange(n_keys) // CHUNK
    scale = (QK_NOPE + QK_ROPE) ** -0.5

    def attend(args):
        qn, qp, qpos = args
        s = jnp.einsum('bqhd,bkhd->bhqk', qn, k_nope).astype(f32) + jnp.einsum('bqhr,bkr->bhqk', qp, kpe_all).astype(f32)
        mask = k_chunk[None, :] <= (qpos // CHUNK)[:, None]
        s = jnp.where(mask[None, None], s * scale, -jnp.inf)
        pr = jax.nn.softmax(s, axis=-1).astype(v.dtype)
        return jnp.einsum('bhqk,bkhd->bqhd', pr, v)

    qb = ATTN_Q_BLOCK if l % ATTN_Q_BLOCK == 0 else l
    nb = l // qb
    qn_b = q_nope.reshape(bsz, nb, qb, MLA_HEADS, QK_NOPE).transpose(1, 0, 2, 3, 4)
    qp_b = q_pe.reshape(bsz, nb, qb, MLA_HEADS, QK_ROPE).transpose(1, 0, 2, 3, 4)
    o = lax.map(attend, (qn_b, qp_b, pos.reshape(nb, qb)))
    o = o.transpose(1, 0, 2, 3, 4).reshape(bsz, l, MLA_HEADS * V_DIM)
    return o @ w_out, ckv, kpe


def peer(xn, wq, keys, u, v):
    bsz, l, d = xn.shape
    t = bsz * l
    tb = PEER_TOKEN_BLOCK if t % PEER_TOKEN_BLOCK == 0 else t

    def block_fn(xt):
        q = (xt @ wq).reshape(tb, PEER_HEADS, 2, PEER_HALF)
        s = jnp.einsum('thcd,hckd->thck', q, keys).astype(jnp.float32)
        s1, i1 = lax.top_k(s[:, :, 0], PEER_TOPK)
        s2, i2 = lax.top_k(s[:, :, 1], PEER_TOPK)
        cand_s = (s1[..., :, None] + s2[..., None, :]).reshape(tb, PEER_HEADS, PEER_TOPK * PEER_TOPK)
        cand_i = (i1[..., :, None] * PEER_NKEYS + i2[..., None, :]).reshape(tb, PEER_HEADS, PEER_TOPK * PEER_TOPK)
        top_s, sel = lax.top_k(cand_s, PEER_TOPK)
        idx = jnp.take_along_axis(cand_i, sel, axis=-1)
        g = jax.nn.softmax(top_s, axis=-1)
        u_sel = jnp.take(u, idx, axis=0)
        v_sel = jnp.take(v, idx, axis=0)
        act = jax.nn.gelu(jnp.einsum('td,thkd->thk', xt, u_sel).astype(jnp.float32))
        return jnp.einsum('thk,thkd->td', (g * act).astype(xt.dtype), v_sel)

    out = lax.map(block_fn, xn.reshape(t // tb, tb, d))
    return out.reshape(bsz, l, d)


def trunk(x, p, conv_st, ssm_st, pool_st, ckv_h, kpe_h, pos0, w):
    h = x
    convs, ssms, pools, ckvs, kpes = [], [], [], [], []
    for i in range(DEPTH):
        hn = rmsnorm(h, w['norm_mix'][i])
        if i % 2 == 0:
            e = i // 2
            out, c_new, s_new, p_new = even_mixer(
                hn, conv_st[e], ssm_st[e], pool_st[e], pos0, w['w_in_e'][e], w['conv_w'][e], w['conv_b'][e],
                w['dt_bias'][e], w['a_log'][e], w['d_skip'][e], w['ssd_norm_w'][e], w['pool_w'][e],
                w['pool_scale'][e], w['w_out_e'][e])
            convs.append(c_new)
            ssms.append(s_new)
            pools.append(p_new)
        else:
            o = i // 2
            out, ckv_new, kpe_new = odd_mixer(
                hn, ckv_h[o], kpe_h[o], pos0, w['w_in_o'][o], w['q_norm'][o], w['kv_norm'][o],
                w['w_uq'][o], w['w_ukv'][o], w['w_out_o'][o])
            ckvs.append(ckv_new)
            kpes.append(kpe_new)
        h = h + out
        h = h + peer(rmsnorm(h, w['norm_ffn'][i]), w['peer_wq'][i], w['peer_keys'][i], w['peer_u'][i], w['peer_v'][i])
        gate = jax.nn.sigmoid((rmsnorm(h, w['ple_norm'][i]) @ w['w_ple_gate'][i]).astype(jnp.float32))
        h = h + (gate * (p[i] @ w['w_ple_proj'][i]).astype(jnp.float32)).astype(h.dtype)
    y = rmsnorm(h, w['final_norm'])
    return y, jnp.stack(convs), jnp.stack(ssms), jnp.stack(pools), jnp.stack(ckvs), jnp.stack(kpes)


def setup_inputs(seed: int = 0) -> dict:
    key = jax.random.key(seed)
    ks = jax.random.split(key, 40)
    nrm = jax.random.normal
    f32 = jnp.float32
    dt0 = jnp.exp(jax.random.uniform(ks[0], (N_EVEN, SSD_HEADS), minval=np.log(1e-3), maxval=np.log(1e-1)))
    return {
        'x_prompt': nrm(ks[1], (BATCH, SEQ, D_MODEL), f32),
        'x_sample': nrm(ks[2], (DEC_BATCH, DEC_SEQ, D_MODEL), f32),
        'state_conv': nrm(ks[3], (N_EVEN, DEC_BATCH, CONV_WIDTH - 1, CONV_CH), f32),
        'state_ssm': 0.1 * nrm(ks[4], (N_EVEN, DEC_BATCH, SSD_HEADS, SSD_HEAD_DIM, SSD_STATE), f32),
        'state_pool': nrm(ks[5], (N_EVEN, DEC_BATCH, POOL_HIST, POOL_WIDTH), f32),
        'cache_ckv': nrm(ks[6], (N_ODD, DEC_BATCH, PAST_LEN, KV_LORA), f32),
        'cache_kpe': nrm(ks[7], (N_ODD, DEC_BATCH, PAST_LEN, QK_ROPE), f32),
        'p_prompt': nrm(ks[8], (DEPTH, BATCH, SEQ, PLE_DIM), f32),
        'p_sample': nrm(ks[9], (DEPTH, DEC_BATCH, DEC_SEQ, PLE_DIM), f32),
        'norm_mix': 1.0 + 0.02 * nrm(ks[10], (DEPTH, D_MODEL), f32),
        'norm_ffn': 1.0 + 0.02 * nrm(ks[11], (DEPTH, D_MODEL), f32),
        'ple_norm': 1.0 + 0.02 * nrm(ks[12], (DEPTH, D_MODEL), f32),
        'final_norm': 1.0 + 0.02 * nrm(ks[13], (D_MODEL,), f32),
        'w_in_e': nrm(ks[14], (N_EVEN, D_MODEL, D_IN_EVEN), f32) * D_MODEL ** -0.5,
        'conv_w': nrm(ks[15], (N_EVEN, CONV_WIDTH, CONV_CH), f32) * CONV_WIDTH ** -0.5,
        'conv_b': 0.02 * nrm(ks[16], (N_EVEN, CONV_CH), f32),
        'dt_bias': dt0 + jnp.log(-jnp.expm1(-dt0)),
        'a_log': jnp.log(jax.random.uniform(ks[17], (N_EVEN, SSD_HEADS), minval=1.0, maxval=16.0)),
        'd_skip': 1.0 + 0.02 * nrm(ks[18], (N_EVEN, SSD_HEADS), f32),
        'ssd_norm_w': 1.0 + 0.02 * nrm(ks[19], (N_EVEN, SSD_WIDTH), f32),
        'pool_w': nrm(ks[20], (N_EVEN, POOL_GROUPS, POOL_GROUP_DIM, POOL_GROUP_DIM), f32) * POOL_GROUP_DIM ** -0.5,
        'pool_scale': 1.0 + 0.05 * nrm(ks[21], (N_EVEN, POOL_WIDTH), f32),
        'w_out_e': nrm(ks[22], (N_EVEN, SSD_WIDTH + POOL_WIDTH, D_MODEL), f32) * (SSD_WIDTH + POOL_WIDTH) ** -0.5,
        'w_in_o': nrm(ks[23], (N_ODD, D_MODEL, D_IN_ODD), f32) * D_MODEL ** -0.5,
        'q_norm': 1.0 + 0.02 * nrm(ks[24], (N_ODD, Q_LORA), f32),
        'kv_norm': 1.0 + 0.02 * nrm(ks[25], (N_ODD, KV_LORA), f32),
        'w_uq': nrm(ks[26], (N_ODD, Q_LORA, MLA_HEADS * (QK_NOPE + QK_ROPE)), f32) * Q_LORA ** -0.5,
        'w_ukv': nrm(ks[27], (N_ODD, KV_LORA, MLA_HEADS * (QK_NOPE + V_DIM)), f32) * KV_LORA ** -0.5,
        'w_out_o': nrm(ks[28], (N_ODD, MLA_HEADS * V_DIM, D_MODEL), f32) * (MLA_HEADS * V_DIM) ** -0.5,
        'peer_wq': nrm(ks[29], (DEPTH, D_MODEL, PEER_HEADS * PEER_QDIM), f32) * D_MODEL ** -0.5,
        'peer_keys': nrm(ks[30], (DEPTH, PEER_HEADS, 2, PEER_NKEYS, PEER_HALF), f32) * PEER_HALF ** -0.5,
        'peer_u': nrm(ks[31], (DEPTH, PEER_EXPERTS, D_MODEL), f32) * D_MODEL ** -0.5,
        'peer_v': 0.1 * nrm(ks[32], (DEPTH, PEER_EXPERTS, D_MODEL), f32),
        'w_ple_proj': nrm(ks[33], (DEPTH, PLE_DIM, D_MODEL), f32) * PLE_DIM ** -0.5,
        'w_ple_gate': nrm(ks[34], (DEPTH, D_MODEL, D_MODEL), f32) * D_MODEL ** -0.5,
    }


def reference(x_prompt, x_sample, state_conv, state_ssm, state_pool, cache_ckv, cache_kpe, p_prompt, p_sample,
              norm_mix, norm_ffn, ple_norm, final_norm, w_in_e, conv_w, conv_b, dt_bias, a_log, d_skip,
              ssd_norm_w, pool_w, pool_scale, w_out_e, w_in_o, q_norm, kv_norm, w_uq, w_ukv, w_out_o,
              peer_wq, peer_keys, peer_u, peer_v, w_ple_proj, w_ple_gate):
    w = dict(norm_mix=norm_mix, norm_ffn=norm_ffn, ple_norm=ple_norm, final_norm=final_norm,
             w_in_e=w_in_e, conv_w=conv_w, conv_b=conv_b, dt_bias=dt_bias, a_log=a_log, d_skip=d_skip,
             ssd_norm_w=ssd_norm_w, pool_w=pool_w, pool_scale=pool_scale, w_out_e=w_out_e,
             w_in_o=w_in_o, q_norm=q_norm, kv_norm=kv_norm, w_uq=w_uq, w_ukv=w_ukv, w_out_o=w_out_o,
             peer_wq=peer_wq, peer_keys=peer_keys, peer_u=peer_u, peer_v=peer_v,
             w_ple_proj=w_ple_proj, w_ple_gate=w_ple_gate)
    b0 = x_prompt.shape[0]
    dtp = x_prompt.dtype
    conv0 = jnp.zeros((N_EVEN, b0, CONV_WIDTH - 1, CONV_CH), dtp)
    ssm0 = jnp.zeros((N_EVEN, b0, SSD_HEADS, SSD_HEAD_DIM, SSD_STATE), jnp.float32)
    pool0 = jnp.zeros((N_EVEN, b0, POOL_HIST, POOL_WIDTH), dtp)
    ckv0 = jnp.zeros((N_ODD, b0, 0, KV_LORA), dtp)
    kpe0 = jnp.zeros((N_ODD, b0, 0, QK_ROPE), dtp)
    y_prompt, conv_p, ssm_p, pool_p, ckv_p, kpe_p = trunk(x_prompt, p_prompt, conv0, ssm0, pool0, ckv0, kpe0, 0, w)
    pos0 = cache_ckv.shape[2]
    y_sample, conv_s, ssm_s, pool_s, ckv_s, kpe_s = trunk(x_sample, p_sample, state_conv, state_ssm, state_pool,
                                                          cache_ckv, cache_kpe, pos0, w)
    return (y_prompt, y_sample, conv_p, ssm_p, pool_p, ckv_p, kpe_p, conv_s, ssm_s, pool_s, ckv_s, kpe_s)
```

```python
import numpy as np
import concourse.bass as bass
import concourse.mybir as mybir
from concourse.bass_utils import run_bass_kernel_spmd

F32 = mybir.dt.float32
BF16 = mybir.dt.bfloat16
I32 = mybir.dt.int32
U32 = mybir.dt.uint32
ALU = mybir.AluOpType
AF = mybir.ActivationFunctionType
AX = mybir.AxisListType

NCORES = 8
D = 1024
NP_ = 2048
NS_ = 32
NTOK = NP_ + NS_
NT = 17
NG = NCORES * NTOK
EPS = 1e-6
EPOCH = 30000
SCALE = 96.0 ** -0.5


def tsz(t):
    return 128 if t < 16 else 32


class Buf:
    __slots__ = ("name", "w", "r")

    def __init__(self, name):
        self.name = name
        self.w = None
        self.r = {}


class Eng:
    def __init__(self, fw, name, eng):
        self.fw = fw
        self.name = name
        self.eng = eng
        self.nsem = 0
        self.count = 0
        self.seen = {}
        self.new_sem()

    def new_sem(self):
        self.sem = self.fw.nc.alloc_semaphore(f"s_{self.name}_{self.nsem}")
        self.nsem += 1
        self.count = 0
        if not hasattr(self, "own"):
            self.own = set()
        self.own.add(self.sem)


class DmaQ:
    def __init__(self, fw, name, E, nsem):
        self.E = E
        self.sems = [fw.nc.alloc_semaphore(f"d_{name}_{i}") for i in range(nsem)]
        self.cnt = [0] * nsem
        self.k = 0


class FW:
    def __init__(self, nc):
        self.nc = nc
        self.PE = Eng(self, "pe", nc.tensor)
        self.ACT = Eng(self, "act", nc.scalar)
        self.DVE = Eng(self, "dve", nc.vector)
        self.POOL = Eng(self, "pool", nc.gpsimd)
        self.SP = Eng(self, "sp", nc.sync)
        self.qSP = DmaQ(self, "sp", self.SP, 24)
        self.qPOOL = DmaQ(self, "pool", self.POOL, 24)
        self.cc_sem = nc.alloc_semaphore("cc_sem")
        self.cc_cnt = 0
        self.nbuf = 0
        self.all_sems = {}

    def buf(self, name=None):
        self.nbuf += 1
        return Buf(name or f"b{self.nbuf}")

    def _deps(self, reads, writes, own=()):
        deps = {}
        for b in reads:
            if b.w is not None and deps.get(b.w[0], 0) < b.w[1]:
                deps[b.w[0]] = b.w[1]
        for b in writes:
            if b.w is not None and b.w[0] not in own and deps.get(b.w[0], 0) < b.w[1]:
                deps[b.w[0]] = b.w[1]
            for s, v in b.r.items():
                if s not in own and deps.get(s, 0) < v:
                    deps[s] = v
        return deps

    def _wait(self, E, deps):
        for s, v in deps.items():
            if E.seen.get(s, 0) < v:
                E.eng.wait_ge(s, v)
                E.seen[s] = v

    def _mark(self, ev, reads, writes):
        s, v = ev
        if self.all_sems.get(s, 0) < v:
            self.all_sems[s] = v
        for b in reads:
            if b.r.get(s, 0) < v:
                b.r[s] = v
        for b in writes:
            b.w = ev
            b.r = {}

    def op(self, E, fn, reads=(), writes=()):
        reads = [b for b in reads if b is not None]
        writes = [b for b in writes if b is not None]
        self._wait(E, self._deps(reads, writes, E.own))
        if E.count >= EPOCH:
            E.new_sem()
        ins = fn()
        E.count += 1
        ins.then_inc(E.sem, 1)
        ev = (E.sem, E.count)
        self._mark(ev, reads, writes)
        return ev

    def dma(self, Q, fn, reads=(), writes=()):
        reads = [b for b in reads if b is not None]
        writes = [b for b in writes if b is not None]
        i = Q.k % len(Q.sems)
        Q.k += 1
        s = Q.sems[i]
        deps = self._deps(reads, writes)
        if Q.cnt[i] > 0 and deps.get(s, 0) < 16 * Q.cnt[i]:
            deps[s] = 16 * Q.cnt[i]
        self._wait(Q.E, deps)
        ins = fn()
        Q.cnt[i] += 1
        ins.then_inc(s, 16)
        ev = (s, 16 * Q.cnt[i])
        self._mark(ev, reads, writes)
        return ev

    def cc(self, fn, reads=(), writes=()):
        self._wait(self.POOL, self._deps(list(reads), list(writes)))
        ins = fn()
        self.cc_cnt += 1
        ins.then_inc(self.cc_sem, 1)
        ev = (self.cc_sem, self.cc_cnt)
        self._mark(ev, list(reads), list(writes))
        return ev

    def barrier(self):
        for E in (self.PE, self.ACT, self.DVE, self.POOL, self.SP):
            self._wait(E, dict(self.all_sems))

    def finish(self):
        self._wait(self.SP, dict(self.all_sems))


class Prog:
    def __init__(self, dbg=None, nlayers=4):
        self.dbg = dbg or {}
        self.nlayers = nlayers
        nc = self.nc = bass.Bass("TRN2", target_bir_lowering=False)
        fw = self.fw = FW(nc)
        self.V, self.A, self.PEe, self.G = fw.DVE, fw.ACT, fw.PE, fw.POOL
        self.dram = {}
        self.dbuf = {}
        self._decl_io()
        self._consts()
        self._run()
        fw.finish()

    def din(self, name, shape, dt=F32):
        self.dram[name] = self.nc.dram_tensor(name, list(shape), dt, kind="ExternalInput").ap()
        self.dbuf[name] = self.fw.buf(name)
        return self.dram[name]

    def dout(self, name, shape, dt=F32):
        self.dram[name] = self.nc.dram_tensor(name, list(shape), dt, kind="ExternalOutput").ap()
        self.dbuf[name] = self.fw.buf(name)
        return self.dram[name]

    def dscr(self, name, shape, dt=F32):
        self.dram[name] = self.nc.dram_tensor(name, list(shape), dt).ap()
        self.dbuf[name] = self.fw.buf(name)
        return self.dram[name]

    def sb(self, name, shape, dt=F32):
        self._uid = getattr(self, "_uid", 0) + 1
        name = f"sb{self._uid}_{name}"
        t = self.nc.alloc_sbuf_tensor(name, list(shape), dt)
        return t, self.fw.buf(name)

    def vop(self, fn, r=(), w=()):
        return self.fw.op(self.V, fn, r, w)

    def aop(self, fn, r=(), w=()):
        return self.fw.op(self.A, fn, r, w)

    def gop(self, fn, r=(), w=()):
        return self.fw.op(self.G, fn, r, w)

    def pop(self, fn, r=(), w=()):
        return self.fw.op(self.PEe, fn, r, w)

    def allgather(self, src, dst):
        nc, dram, dbuf = self.nc, self.dram, self.dbuf
        if self.dbg.get("_stubcc"):
            rows = dram[src].shape[0]
            return self.ld(dram[dst][0:rows], dram[src], r=[dbuf[src]], w=[dbuf[dst]])
        return self.fw.cc(lambda: nc.gpsimd.collective_compute("AllGather", ALU.bypass, replica_groups=[list(range(NCORES))],
                                                               ins=[dram[src]], outs=[dram[dst]]),
                          reads=[dbuf[src]], writes=[dbuf[dst]])

    def ld(self, out_ap, in_ap, r=(), w=()):
        nc = self.nc
        return self.fw.dma(self.fw.qSP, lambda: nc.sync.dma_start(out=out_ap, in_=in_ap), r, w)

    def ldc(self, out_ap, in_ap, r=(), w=()):
        nc = self.nc
        return self.fw.dma(self.fw.qPOOL, lambda: nc.gpsimd.dma_start(out=out_ap, in_=in_ap), r, w)

    def _decl_io(self):
        din, dout, dscr = self.din, self.dout, self.dscr
        din("xs", [NTOK, D])
        din("pT", [4, 256, NTOK])
        din("ncols", [128, 3, 4, 8])
        din("final_norm", [1, D])
        din("norm_ffn", [4, D])
        din("w_in_e", [2, D, 3600])
        din("colE", [2, 128, 68])
        din("rowE", [2, 3, 16])
        din("ssd_norm_w", [2, D])
        din("pool_w", [2, 4, 256, 256])
        din("w_out_e", [2, 2048, D])
        din("w_in_o", [2, D, 576])
        din("qkv_norm", [2, 2, 256])
        din("wq_c", [2, 256, 2, 128])
        din("wkv_c", [2, 256, 2, 128])
        din("w_out_o", [2, D, D])
        din("peer_wq", [4, D, 2048])
        din("keysT", [4, 16, 128, 128])
        npe = 128 if self.dbg.get("_nopeer") else (16384 if self.dbg.get("_smallpeer") else 4 * 16384)
        din("peer_uv", [npe, 2 * D])
        din("w_ple_proj", [4, 256, D])
        din("w_ple_gate", [4, D, D])
        din("cache_ckvT", [2, 8, 256, 2048])
        din("cache_kpeT", [2, 8, 32, 2048])
        din("st_conv", [2, 128, 12, 3])
        din("st_pool", [2, 128, 8, 15])
        din("st_ssmT", [2, 128, D])
        din("rope_t", [NTOK, 2, 32])
        din("rope_f", [2, 32, NG])
        din("pcorr", [1, 4 * 15])
        din("cmask", [1, 64 + 8 + 8])
        din("gidx", [128, NT * 8], I32)
        dout("y", [NTOK, D])
        dout("ckv_out", [2, NTOK, 256])
        dout("kpe_out", [2, NTOK, 32])
        dout("conv_out", [2, 2, 3, 1536])
        dout("pool_out", [2, 2, 15, D])
        dout("ssm_out", [2, 2, 128, D])
        for k, shp in self.dbg.items():
            if not k.startswith("_"):
                dout(k, shp)
        dscr("peer_uvb", [npe, 2 * D], BF16)
        dscr("ylocal", [NTOK, D])
        dscr("ypT_d", [128, 8, NTOK], BF16)
        dscr("halo_src", [128, 156])
        dscr("halo_all", [NCORES * 128, 156])
        dscr("st_src", [128, 1040])
        dscr("st_all", [NCORES * 128, 1040])
        dscr("pay", [544, NTOK], BF16)
        dscr("payall", [NCORES * 544, NTOK], BF16)
        dscr("o_d", [NG, 128], BF16)
        dscr("oall", [NCORES * NG, 128], BF16)

    def _consts(self):
        nc = self.nc
        self.h, _ = self.sb("h", [128, NT, D])
        self.hb = [self.fw.buf(f"h{t}") for t in range(NT)]
        self.ident_f, self.b_idf = self.sb("ident_f", [128, 128])
        self.ident_b, self.b_idb = self.sb("ident_b", [128, 128], BF16)
        self.triU, self.b_tri = self.sb("triU", [128, 128])
        self.negm, self.b_negm = self.sb("negm", [128, 128])
        self.ones_f, self.b_ones = self.sb("ones_f", [128, 128])
        self.iota16, self.b_iota = self.sb("iota16", [128, 16])
        self.ncols, self.b_ncols = self.sb("ncols_sb", [128, 3, 4, 8])
        g = nc.gpsimd
        self.gop(lambda: g.memset(self.ident_f[:], 1.0), w=[self.b_idf])
        self.gop(lambda: g.affine_select(out=self.ident_f[:], in_=self.ident_f[:], pattern=[[-1, 128]],
                                         compare_op=ALU.is_equal, fill=0.0, base=0, channel_multiplier=1),
                 r=[self.b_idf], w=[self.b_idf])
        self.vop(lambda: nc.vector.tensor_copy(out=self.ident_b[:], in_=self.ident_f[:]), r=[self.b_idf], w=[self.b_idb])
        self.gop(lambda: g.memset(self.triU[:], 1.0), w=[self.b_tri])
        self.gop(lambda: g.affine_select(out=self.triU[:], in_=self.triU[:], pattern=[[1, 128]],
                                         compare_op=ALU.is_ge, fill=0.0, base=0, channel_multiplier=-1),
                 r=[self.b_tri], w=[self.b_tri])
        self.gop(lambda: g.memset(self.negm[:], 0.0), w=[self.b_negm])
        self.gop(lambda: g.affine_select(out=self.negm[:], in_=self.negm[:], pattern=[[1, 128]],
                                         compare_op=ALU.is_ge, fill=-30000.0, base=0, channel_multiplier=-1),
                 r=[self.b_negm], w=[self.b_negm])
        self.gop(lambda: g.memset(self.ones_f[:], 1.0), w=[self.b_ones])
        self.gop(lambda: g.iota(self.iota16[:], pattern=[[1, 16]], base=0, channel_multiplier=0,
                                allow_small_or_imprecise_dtypes=True), w=[self.b_iota])
        self.ld(self.ncols[:], self.dram["ncols"], w=[self.b_ncols])
        self.psg = []
        for i in range(6):
            t = nc.alloc_psum_tensor(f"psg{i}", [128, 512], F32)
            self.psg.append((t, self.fw.buf(f"psg{i}")))
        self.pst = []
        for i in range(2):
            t = nc.alloc_psum_tensor(f"pst{i}", [128, 8, 128], BF16)
            self.pst.append((t, self.fw.buf(f"pst{i}")))
        self.pst_k = 0

    def next_pst(self):
        self.pst_k += 1
        return self.pst[self.pst_k % 2]

    def rms_stats(self, src_ap, src_b, T, rs, rs_b, junk, junk_b, n=D, col=0):
        nc = self.nc
        self.aop(lambda: nc.scalar.activation(out=junk[:T, :n], in_=src_ap, func=AF.Square,
                                              accum_out=rs[:T, col:col + 1]), r=[src_b], w=[junk_b, rs_b])
        self.aop(lambda: nc.scalar.activation(out=rs[:T, col:col + 1], in_=rs[:T, col:col + 1], func=AF.Sqrt,
                                              scale=1.0 / n, bias=EPS), r=[rs_b], w=[rs_b])
        self.vop(lambda: nc.vector.reciprocal(out=rs[:T, col:col + 1], in_=rs[:T, col:col + 1]), r=[rs_b], w=[rs_b])

    def norm_T(self, t, which, l, dstT_ap, dst_b, tmp):
        nc = self.nc
        T = tsz(t)
        rs, rs_b, junk, junk_b, xn, xn_b = tmp
        self.rms_stats(self.h[:T, t, :], self.hb[t], T, rs, rs_b, junk, junk_b)
        self.vop(lambda: nc.vector.tensor_scalar(out=xn[:T, :], in0=self.h[:T, t, :], scalar1=rs[:T, 0:1],
                                                 scalar2=None, op0=ALU.mult), r=[self.hb[t], rs_b], w=[xn_b])
        pt, pt_b = self.next_pst()
        for kc in range(8):
            self.pop(lambda kc=kc: nc.tensor.transpose(out=pt[:, kc, :T], in_=xn[:T, kc * 128:(kc + 1) * 128],
                                                       identity=self.ident_b[:T, :T]),
                     r=[xn_b, self.b_idb], w=[pt_b])
        wcol = self.ncols[:, which, l, :]
        self.vop(lambda: nc.vector.tensor_tensor(out=dstT_ap, in0=pt[:, :, :T],
                                                 in1=wcol.unsqueeze(2).to_broadcast([128, 8, T]), op=ALU.mult),
                 r=[pt_b, self.b_ncols], w=[dst_b])

    def norm_tmp(self, stack, pfx):
        rs, rs_b = self.phase_alloc(stack, pfx + "_rs", [128, 4])
        xn, xn_b = self.phase_alloc(stack, pfx + "_xn", [128, D], BF16)
        return (rs, rs_b, xn, xn_b, xn, xn_b)

    def dbg_dump_h(self, name):
        if name in self.dbg:
            for t in range(NT):
                T = tsz(t)
                self.ld(self.dram[name][t * 128:t * 128 + T, :], self.h[:T, t, :], r=[self.hb[t]], w=[self.dbuf[name]])

    def _run(self):
        nc = self.nc
        self.vop(lambda: nc.vector.memset(self.h[:, NT - 1, :], 0.0), w=[self.hb[NT - 1]])
        for t in range(NT):
            T = tsz(t)
            self.ld(self.h[:T, t, :], self.dram["xs"][t * 128:t * 128 + T, :], r=[self.dbuf["xs"]], w=[self.hb[t]])
        if not self.dbg.get("_nopeer"):
            npe = self.dram["peer_uv"].shape[0]
            CH = 1024
            self.cvt_b = {}
            for r0 in range(0, npe, CH):
                lb = self.cvt_b.setdefault(r0 // 16384, self.fw.buf())
                self.ldc(self.dram["peer_uvb"][r0:r0 + CH, :], self.dram["peer_uv"][r0:r0 + CH, :], r=[self.dbuf["peer_uv"]], w=[lb])
        for l in range(self.nlayers):
            if l % 2 == 0:
                self.even_layer(l // 2, l)
            else:
                self.odd_layer(l // 2, l)
            self.dbg_dump_h(f"dbg_mix{l}")
            self.peer_layer(l)
            self.dbg_dump_h(f"dbg_peer{l}")
            self.ple_layer(l)
            self.dbg_dump_h(f"dbg_ple{l}")
        self.final_norm()

    def final_norm(self):
        nc = self.nc
        self.fw.barrier()
        with nc.sbuf_tensor("fn_row", [128, D], F32) as wrow, nc.sbuf_tensor("fn_rs", [128, 4], F32) as rs, \
                nc.sbuf_tensor("fn_junk", [128, D], BF16) as junk, nc.sbuf_tensor("fn_y", [128, 2, D], F32) as yb:
            b_w, b_rs, b_j = self.fw.buf(), self.fw.buf(), self.fw.buf()
            b_y = [self.fw.buf(), self.fw.buf()]
            self.ld(wrow[:], self.dram["final_norm"].to_broadcast([128, D]), w=[b_w])
            for t in range(NT):
                T = tsz(t)
                self.rms_stats(self.h[:T, t, :], self.hb[t], T, rs, b_rs, junk, b_j)
                k = t % 2
                self.vop(lambda t=t, T=T, k=k: nc.vector.scalar_tensor_tensor(
                    out=yb[:T, k, :], in0=self.h[:T, t, :], scalar=rs[:T, 0:1], in1=wrow[:T, :],
                    op0=ALU.mult, op1=ALU.mult), r=[self.hb[t], b_rs, b_w], w=[b_y[k]])
                self.ld(self.dram["y"][t * 128:t * 128 + T, :], yb[:T, k, :], r=[b_y[k]], w=[self.dbuf["y"]])
            self.fw.barrier()

    def phase_alloc(self, stack, name, shape, dt=F32):
        self._uid = getattr(self, "_uid", 0) + 1
        name = f"sb{self._uid}_{name}"
        t = stack.enter_context(self.nc.sbuf_tensor(name, list(shape), dt))
        return t, self.fw.buf(name)

    def mm_acc(self, out_ap, out_b, pairs, reads):
        nc = self.nc
        n = len(pairs)
        for i, (lt, rh) in enumerate(pairs):
            self.pop(lambda lt=lt, rh=rh, i=i: nc.tensor.matmul(out_ap, lhsT=lt, rhs=rh, start=(i == 0), stop=(i == n - 1)),
                     r=reads, w=[out_b])

    def even_layer(self, e, l):
        from contextlib import ExitStack
        nc, fw = self.nc, self.fw
        dram, dbuf = self.dram, self.dbuf
        wie = dram["w_in_e"][e].rearrange("(kc p) n -> p kc n", p=128)
        b_wie = dbuf["w_in_e"]
        fw.barrier()
        with ExitStack() as L:
            colE, b_colE = self.phase_alloc(L, "colE", [128, 68])
            rowE, b_rowE = self.phase_alloc(L, "rowE", [128, 3, 16])
            ssdw, b_ssdw = self.phase_alloc(L, "ssdw", [128, D])
            pcorr, b_pcorr = self.phase_alloc(L, "pcorr", [128, 4, 15])
            cmask, b_cmask = self.phase_alloc(L, "cmask", [128, 80])
            CT_all, b_CT = self.phase_alloc(L, "CT_all", [128, 2, NTOK], BF16)
            ect, b_ect = self.phase_alloc(L, "ect", [128, NT, 16])
            uhalo = {}
            rhalo = {}
            for sq in "ps":
                uhalo[sq] = self.phase_alloc(L, "uhalo" + sq, [128, 8, 15])
                rhalo[sq] = self.phase_alloc(L, "rhalo" + sq, [128, 12, 3])
            hin = {}
            for sq in "ps":
                hin[sq] = self.phase_alloc(L, "hin" + sq, [128, D])
            hinb = {}
            for sq in "ps":
                hinb[sq] = self.phase_alloc(L, "hinb" + sq, [128, D], BF16)
            Lst = {}
            rcl = {}
            for sq in "ps":
                Lst[sq] = self.phase_alloc(L, "Lst" + sq, [128, D])
                rcl[sq] = self.phase_alloc(L, "rcl" + sq, [128, 16])
            arow, b_arow = self.phase_alloc(L, "arow", [128, 16])
            self.ld(colE[:], dram["colE"][e], w=[b_colE])
            self.ld(rowE[:], dram["rowE"][e:e + 1].to_broadcast([128, 3, 16]), w=[b_rowE])
            self.ld(ssdw[:], dram["ssd_norm_w"][e:e + 1, :].to_broadcast([128, D]), w=[b_ssdw])
            self.ld(pcorr[:], dram["pcorr"].rearrange("o (g k) -> o g k", g=4).to_broadcast([128, 4, 15]), w=[b_pcorr])
            self.ld(cmask[:], dram["cmask"].to_broadcast([128, 80]), w=[b_cmask])
            self.aop(lambda: nc.scalar.activation(out=arow[:], in_=rowE[:, 1, :], func=AF.Exp), r=[b_rowE], w=[b_arow])
            self.vop(lambda: nc.vector.tensor_scalar(out=arow[:], in0=arow[:], scalar1=-1.0, scalar2=None, op0=ALU.mult),
                     r=[b_arow], w=[b_arow])
            self.ld(uhalo["s"][0][:], dram["st_pool"][e], w=[uhalo["s"][1]])
            self.ld(rhalo["s"][0][:], dram["st_conv"][e], w=[rhalo["s"][1]])
            self.ld(hin["s"][0][:], dram["st_ssmT"][e], w=[hin["s"][1]])
            groups = [("p", [0, 1, 2, 3]), ("p", [4, 5, 6, 7]), ("p", [8, 9, 10, 11]), ("p", [12, 13, 14, 15]), ("s", [16])]

            with ExitStack() as Ph:
                W_u, b_Wu = self.phase_alloc(Ph, "W_u", [128, 8, 1024], BF16)
                W_x, b_Wx = self.phase_alloc(Ph, "W_x", [128, 8, 1536], BF16)
                hT15, b_hT15 = self.phase_alloc(Ph, "hT15", [128, 8, 128], BF16)
                hsrc, b_hsrc = self.phase_alloc(Ph, "hsrc", [128, 156])
                hall, b_hall = self.phase_alloc(Ph, "hall", [128, 8, 156])
                tmp = self.norm_tmp(Ph, "p0")
                self.ldc(W_u[:], wie[:, :, 2576:3600], r=[b_wie], w=[b_Wu])
                self.ldc(W_x[:], wie[:, :, 1024:2560], r=[b_wie], w=[b_Wx])
                self.norm_T(15, 0, l, hT15[:, :, :128], b_hT15, tmp)
                for c in range(8):
                    ps, psb = self.psg[c % 6]
                    self.mm_acc(ps[:, 0:15], psb, [(W_u[:, kc, c * 128:(c + 1) * 128], hT15[:, kc, 113:128]) for kc in range(8)],
                                [b_Wu, b_hT15])
                    self.aop(lambda c=c, ps=ps: nc.scalar.copy(out=hsrc[:, c * 15:(c + 1) * 15], in_=ps[:, 0:15]), r=[psb], w=[b_hsrc])
                for c in range(12):
                    ps, psb = self.psg[c % 6]
                    self.mm_acc(ps[:, 0:3], psb, [(W_x[:, kc, c * 128:(c + 1) * 128], hT15[:, kc, 125:128]) for kc in range(8)],
                                [b_Wx, b_hT15])
                    self.aop(lambda c=c, ps=ps: nc.scalar.copy(out=hsrc[:, 120 + c * 3:120 + (c + 1) * 3], in_=ps[:, 0:3]), r=[psb], w=[b_hsrc])
                self.ld(dram["halo_src"], hsrc[:], r=[b_hsrc], w=[dbuf["halo_src"]])
                self.allgather("halo_src", "halo_all")
                self.ld(hall[:], dram["halo_all"].rearrange("(r p) f -> p r f", p=128), r=[dbuf["halo_all"]], w=[b_hall])
                for r in range(8):
                    if r == 0:
                        self.vop(lambda: nc.vector.tensor_scalar(out=hsrc[:], in0=hall[:, 0, :], scalar1=cmask[:, 72:73], scalar2=None,
                                                                 op0=ALU.mult), r=[b_hall, b_cmask], w=[b_hsrc])
                    else:
                        self.vop(lambda r=r: nc.vector.scalar_tensor_tensor(out=hsrc[:], in0=hall[:, r, :], scalar=cmask[:, 72 + r:73 + r],
                                                                          in1=hsrc[:], op0=ALU.mult, op1=ALU.add),
                                 r=[b_hall, b_cmask, b_hsrc], w=[b_hsrc])
                self.vop(lambda: nc.vector.tensor_copy(out=uhalo["p"][0][:], in_=hsrc[:, 0:120].rearrange("p (c k) -> p c k", c=8)),
                         r=[b_hsrc], w=[uhalo["p"][1]])
                self.vop(lambda: nc.vector.tensor_copy(out=rhalo["p"][0][:], in_=hsrc[:, 120:156].rearrange("p (c k) -> p c k", c=12)),
                         r=[b_hsrc], w=[rhalo["p"][1]])
                fw.barrier()

            with ExitStack() as Ph:
                W_u, b_Wu = self.phase_alloc(Ph, "W_u", [128, 8, 1024], BF16)
                poolw, b_pw = self.phase_alloc(Ph, "poolw", [128, 4, 2, 256], BF16)
                hnT, b_hnT = self.phase_alloc(Ph, "hnT", [128, 8, 512], BF16)
                uT, b_uT = self.phase_alloc(Ph, "uT", [128, 8, 15 + 512])
                S0, b_S0 = self.phase_alloc(Ph, "S0", [128, 15 + 512])
                S1, b_S1 = self.phase_alloc(Ph, "S1", [128, 15 + 512])
                pooledT, b_pT = self.phase_alloc(Ph, "pooledT", [128, 2, 512], BF16)
                ypT, b_ypT = self.phase_alloc(Ph, "ypT", [128, 8, 512], BF16)
                po_sb, b_po = self.phase_alloc(Ph, "po_sb", [128, D])
                tmp = self.norm_tmp(Ph, "p1a")
                self.ldc(W_u[:], wie[:, :, 2576:3600], r=[b_wie], w=[b_Wu])
                self.ldc(poolw[:], dram["pool_w"][e].rearrange("g (cc p) d -> p g cc d", p=128), r=[dbuf["pool_w"]], w=[b_pw])
                first_p = True
                for gidx_, (sq, tiles) in enumerate(groups):
                    N = sum(tsz(t) for t in tiles)
                    tok0 = tiles[0] * 128
                    L_ = 15 + N
                    off = 0
                    for t in tiles:
                        self.norm_T(t, 0, l, hnT[:, :, off:off + tsz(t)], b_hnT, tmp)
                        off += tsz(t)
                    if first_p or sq == "s":
                        self.vop(lambda sq=sq: nc.vector.tensor_copy(out=uT[:, :, 0:15], in_=uhalo[sq][0][:]), r=[uhalo[sq][1]], w=[b_uT])
                    for c in range(8):
                        ps, psb = self.psg[c % 6]
                        self.mm_acc(ps[:, 0:N], psb, [(W_u[:, kc, c * 128:(c + 1) * 128], hnT[:, kc, 0:N]) for kc in range(8)], [b_Wu, b_hnT])
                        self.aop(lambda c=c, ps=ps, N=N: nc.scalar.copy(out=uT[:, c, 15:15 + N], in_=ps[:, 0:N]), r=[psb], w=[b_uT])
                    for gi in range(4):
                        w_ = 2 ** (gi + 1)
                        for cc in range(2):
                            c = 2 * gi + cc
                            a = uT[:, c, :]
                            self.vop(lambda a=a, L_=L_: nc.vector.tensor_tensor(out=S0[:, 1:L_], in0=a[:, 1:L_], in1=a[:, 0:L_ - 1], op=ALU.add),
                                     r=[b_uT], w=[b_S0])
                            cur, cur_b, oth, oth_b = S0, b_S0, S1, b_S1
                            sh = 2
                            while sh < w_:
                                lo = 2 * sh - 1
                                self.vop(lambda cur=cur, oth=oth, lo=lo, sh=sh, L_=L_: nc.vector.tensor_tensor(
                                    out=oth[:, lo:L_], in0=cur[:, lo:L_], in1=cur[:, lo - sh:L_ - sh], op=ALU.add), r=[cur_b], w=[oth_b])
                                cur, cur_b, oth, oth_b = oth, oth_b, cur, cur_b
                                sh *= 2
                            if first_p and sq == "p":
                                self.vop(lambda cur=cur, gi=gi: nc.vector.tensor_tensor(out=cur[:, 15:30], in0=cur[:, 15:30], in1=pcorr[:, gi, :], op=ALU.mult),
                                         r=[cur_b, b_pcorr], w=[cur_b])
                            self.vop(lambda cur=cur, a=a, cc=cc, N=N, w_=w_: nc.vector.scalar_tensor_tensor(
                                out=pooledT[:, cc, 0:N], in0=cur[:, 15:15 + N], scalar=1.0 / w_, in1=a[:, 15:15 + N],
                                op0=ALU.mult, op1=ALU.subtract), r=[cur_b, b_uT], w=[b_pT])
                        for dc in range(2):
                            ps, psb = self.psg[(2 * gi + dc) % 6]
                            self.mm_acc(ps[:, 0:N], psb, [(poolw[:, gi, cc, dc * 128:(dc + 1) * 128], pooledT[:, cc, 0:N]) for cc in range(2)],
                                        [b_pw, b_pT])
                            cidx = 2 * gi + dc
                            self.vop(lambda ps=ps, cidx=cidx, N=N: nc.vector.tensor_scalar(out=ypT[:, cidx, 0:N], in0=ps[:, 0:N],
                                                                                       scalar1=colE[:, 60 + cidx:61 + cidx], scalar2=None,
                                                                                       op0=ALU.mult), r=[psb, b_colE], w=[b_ypT])
                    self.ld(dram["ypT_d"][:, :, tok0:tok0 + N], ypT[:, :, 0:N], r=[b_ypT], w=[dbuf["ypT_d"]])
                    last_of_seq = (sq == "s") or (gidx_ == 3)
                    if last_of_seq:
                        self.vop(lambda sq=sq, N=N: nc.vector.tensor_copy(out=uhalo[sq][0][:], in_=uT[:, :, N:N + 15]), r=[b_uT], w=[uhalo[sq][1]])
                        ps, psb = self.psg[0]
                        ps2, psb2 = self.psg[1]
                        for c in range(8):
                            pp, ppb = (ps, psb) if c < 4 else (ps2, psb2)
                            self.pop(lambda c=c, pp=pp, sq=sq: nc.tensor.transpose(out=pp[0:15, (c % 4) * 128:(c % 4 + 1) * 128],
                                                                                 in_=uhalo[sq][0][:, c, :], identity=self.ident_f[:]),
                                     r=[uhalo[sq][1], self.b_idf], w=[ppb])
                        self.aop(lambda: nc.scalar.copy(out=po_sb[0:15, 0:512], in_=ps[0:15, :]), r=[psb], w=[b_po])
                        self.aop(lambda: nc.scalar.copy(out=po_sb[0:15, 512:1024], in_=ps2[0:15, :]), r=[psb2], w=[b_po])
                        si = 0 if sq == "p" else 1
                        self.ld(dram["pool_out"][e, si], po_sb[0:15, :], r=[b_po], w=[dbuf["pool_out"]])
                    else:
                        self.vop(lambda N=N: nc.vector.tensor_copy(out=uT[:, :, 0:15], in_=uT[:, :, N:N + 15]), r=[b_uT], w=[b_uT])
                    if sq == "p":
                        first_p = False
                fw.barrier()

            with ExitStack() as Ph:
                W_x, b_Wx = self.phase_alloc(Ph, "W_x", [128, 8, 1536], BF16)
                W_dt, b_Wdt = self.phase_alloc(Ph, "W_dt", [128, 8, 16], BF16)
                hnT, b_hnT = self.phase_alloc(Ph, "hnT", [128, 8, 512], BF16)
                raw = [self.phase_alloc(Ph, f"raw{i}", [128, 3 + 512]) for i in range(2)]
                cacc = [self.phase_alloc(Ph, f"cacc{i}", [128, 512]) for i in range(2)]
                xcT, b_xcT = self.phase_alloc(Ph, "xcT", [128, 12, 512], BF16)
                x_tms = [self.phase_alloc(Ph, f"x_tm{i}", [128, 1, D], BF16) for i in range(2)]
                B_tms = [self.phase_alloc(Ph, f"B_tm{i}", [128, 1, 256], BF16) for i in range(2)]
                dtt, b_dt = self.phase_alloc(Ph, "dtt", [128, 16])
                dA, b_dA = self.phase_alloc(Ph, "dA", [128, 16])
                cum, b_cum = self.phase_alloc(Ph, "cum", [128, 16])
                clb, b_clb = self.phase_alloc(Ph, "clb", [128, 16])
                ecum, b_ecum = self.phase_alloc(Ph, "ecum", [128, 16])
                wgt, b_wgt = self.phase_alloc(Ph, "wgt", [128, 16])
                edec, b_edec = self.phase_alloc(Ph, "edec", [128, 16])
                sm16, b_sm16 = self.phase_alloc(Ph, "sm16", [128, 16])
                Dms = [self.phase_alloc(Ph, f"Dm{i}", [128, 512]) for i in range(2)]
                dif = [self.phase_alloc(Ph, f"dif{i}", [128, 512]) for i in range(2)]
                MT, b_MT = self.phase_alloc(Ph, "MT", [128, 2048], BF16)
                y1, b_y1 = self.phase_alloc(Ph, "y1", [128, D])
                y2, b_y2 = self.phase_alloc(Ph, "y2", [128, D])
                xw, b_xw = self.phase_alloc(Ph, "xw", [128, D], BF16)
                hT, b_hT = self.phase_alloc(Ph, "hT", [128, D])
                hTb, b_hTb = self.phase_alloc(Ph, "hTb", [128, D], BF16)
                tmp = self.norm_tmp(Ph, "p1b")
                self.ldc(W_x[:], wie[:, :, 1024:2560], r=[b_wie], w=[b_Wx])
                self.ldc(W_dt[:], wie[:, :, 2560:2576], r=[b_wie], w=[b_Wdt])
                cw = lambda c, j: colE[:, c * 4 + j:c * 4 + j + 1]
                cur_seq = None
                for gidx_, (sq, tiles) in enumerate(groups):
                    N = sum(tsz(t) for t in tiles)
                    tok0 = tiles[0] * 128
                    if sq != cur_seq:
                        cur_seq = sq
                        self.vop(lambda: nc.vector.memset(hT[:], 0.0), w=[b_hT])
                        self.vop(lambda: nc.vector.memset(hTb[:], 0.0), w=[b_hTb])
                        self.vop(lambda sq=sq: nc.vector.memset(rcl[sq][0][:], 0.0), w=[rcl[sq][1]])
                    off = 0
                    for t in tiles:
                        self.norm_T(t, 0, l, hnT[:, :, off:off + tsz(t)], b_hnT, tmp)
                        off += tsz(t)
                    for c in range(12):
                        ps, psb = self.psg[c % 6]
                        rw, rwb = raw[c % 2]
                        ca, cab = cacc[c % 2]
                        self.mm_acc(ps[:, 0:N], psb, [(W_x[:, kc, c * 128:(c + 1) * 128], hnT[:, kc, 0:N]) for kc in range(8)], [b_Wx, b_hnT])
                        self.gop(lambda rw=rw, c=c, sq=sq: nc.gpsimd.tensor_copy(out=rw[:, 0:3], in_=rhalo[sq][0][:, c, :]), r=[rhalo[sq][1]], w=[rwb])
                        self.aop(lambda rw=rw, ps=ps, N=N: nc.scalar.copy(out=rw[:, 3:3 + N], in_=ps[:, 0:N]), r=[psb], w=[rwb])
                        self.gop(lambda rw=rw, c=c, sq=sq, N=N: nc.gpsimd.tensor_copy(out=rhalo[sq][0][:, c, :], in_=rw[:, N:N + 3]), r=[rwb], w=[rhalo[sq][1]])
                        self.vop(lambda rw=rw, ca=ca, c=c, N=N: nc.vector.tensor_scalar(out=ca[:, 0:N], in0=rw[:, 0:N], scalar1=cw(c, 0), scalar2=colE[:, 48 + c:49 + c],
                                                                                   op0=ALU.mult, op1=ALU.add), r=[rwb, b_colE], w=[cab])
                        for j in range(1, 4):
                            self.vop(lambda rw=rw, ca=ca, c=c, j=j, N=N: nc.vector.scalar_tensor_tensor(out=ca[:, 0:N], in0=rw[:, j:j + N], scalar=cw(c, j), in1=ca[:, 0:N],
                                                                                                   op0=ALU.mult, op1=ALU.add), r=[rwb, b_colE, cab], w=[cab])
                        self.aop(lambda ca=ca, c=c, N=N: nc.scalar.activation(out=xcT[:, c, 0:N], in_=ca[:, 0:N], func=AF.Silu), r=[cab], w=[b_xcT])
                    self.gop(lambda tok0=tok0, N=N: nc.gpsimd.tensor_copy(out=CT_all[:, :, tok0:tok0 + N], in_=xcT[:, 10:12, 0:N]), r=[b_xcT], w=[b_CT])
                    off = 0
                    for ti, t in enumerate(tiles):
                        T = tsz(t)
                        cs = slice(off, off + T)
                        x_tm, b_xtm = x_tms[t % 2]
                        B_tm, b_Btm = B_tms[t % 2]
                        ti = 0
                        pt, ptb = self.next_pst()
                        for c in range(8):
                            self.pop(lambda c=c, pt=pt, cs=cs, T=T: nc.tensor.transpose(out=pt[:T, c, :], in_=xcT[:, c, cs], identity=self.ident_b[:]),
                                     r=[b_xcT, self.b_idb], w=[ptb])
                        self.aop(lambda pt=pt, ti=ti, T=T: nc.scalar.copy(out=x_tm[:T, ti, :], in_=pt[:T].rearrange("p c k -> p (c k)")), r=[ptb], w=[b_xtm])
                        pt2, ptb2 = self.next_pst()
                        for gi in range(2):
                            self.pop(lambda gi=gi, pt2=pt2, cs=cs, T=T: nc.tensor.transpose(out=pt2[:T, gi, :], in_=xcT[:, 8 + gi, cs], identity=self.ident_b[:]),
                                     r=[b_xcT, self.b_idb], w=[ptb2])
                        self.aop(lambda pt2=pt2, ti=ti, T=T: nc.scalar.copy(out=B_tm[:T, ti, :], in_=pt2[:T, 0:2, :].rearrange("p c k -> p (c k)")), r=[ptb2], w=[b_Btm])
                        ps, psb = self.psg[0]
                        self.mm_acc(ps[:T, 0:16], psb, [(hnT[:, kc, cs], W_dt[:, kc, :]) for kc in range(8)], [b_hnT, b_Wdt])
                        self.vop(lambda ps=ps, T=T: nc.vector.tensor_tensor(out=dtt[:T, :], in0=ps[:T, 0:16], in1=rowE[:T, 0, :], op=ALU.add), r=[psb, b_rowE], w=[b_dt])
                        self.aop(lambda T=T: nc.scalar.activation(out=dtt[:T, :], in_=dtt[:T, :], func=AF.Exp), r=[b_dt], w=[b_dt])
                        self.aop(lambda T=T: nc.scalar.activation(out=dtt[:T, :], in_=dtt[:T, :], func=AF.Ln, bias=1.0), r=[b_dt], w=[b_dt])
                        self.vop(lambda T=T: nc.vector.tensor_tensor(out=dA[:T, :], in0=dtt[:T, :], in1=arow[:T, :], op=ALU.mult), r=[b_dt, b_arow], w=[b_dA])
                        ps0, psb0 = self.psg[0]
                        self.pop(lambda T=T: nc.tensor.matmul(ps0[:T, 16:32], lhsT=self.triU[:T, :T], rhs=dA[:T, :], start=True, stop=True), r=[self.b_tri, b_dA], w=[psb0])
                        self.pop(lambda T=T: nc.tensor.matmul(ps0[:, 32:48], lhsT=self.ones_f[:T, :], rhs=dA[:T, :], start=True, stop=True), r=[self.b_ones, b_dA], w=[psb0])
                        self.vop(lambda T=T: nc.vector.tensor_copy(out=cum[:T, :], in_=ps0[:T, 16:32]), r=[psb0], w=[b_cum])
                        self.vop(lambda: nc.vector.tensor_copy(out=clb[:, :], in_=ps0[:, 32:48]), r=[psb0], w=[b_clb])
                        self.aop(lambda T=T: nc.scalar.activation(out=ecum[:T, :], in_=cum[:T, :], func=AF.Exp), r=[b_cum], w=[b_ecum])
                        self.vop(lambda T=T, sq=sq: nc.vector.tensor_tensor(out=sm16[:T, :], in0=cum[:T, :], in1=rcl[sq][0][:T, :], op=ALU.add), r=[b_cum, rcl[sq][1]], w=[b_sm16])
                        self.aop(lambda T=T, t=t: nc.scalar.activation(out=ect[:T, t, :], in_=sm16[:T, :], func=AF.Exp), r=[b_sm16], w=[b_ect])
                        self.vop(lambda T=T: nc.vector.tensor_tensor(out=wgt[:T, :], in0=clb[:T, :], in1=cum[:T, :], op=ALU.subtract), r=[b_clb, b_cum], w=[b_wgt])
                        self.aop(lambda T=T: nc.scalar.activation(out=wgt[:T, :], in_=wgt[:T, :], func=AF.Exp), r=[b_wgt], w=[b_wgt])
                        self.vop(lambda T=T: nc.vector.tensor_tensor(out=wgt[:T, :], in0=wgt[:T, :], in1=dtt[:T, :], op=ALU.mult), r=[b_wgt, b_dt], w=[b_wgt])
                        self.aop(lambda: nc.scalar.activation(out=edec[:, :], in_=clb[:, :], func=AF.Exp), r=[b_clb], w=[b_edec])
                        psC, psCb = self.psg[1]
                        for gi in range(2):
                            self.pop(lambda gi=gi, cs=cs, T=T: nc.tensor.matmul(psC[:T, gi * 128:gi * 128 + T], lhsT=xcT[:, 8 + gi, cs], rhs=xcT[:, 10 + gi, cs],
                                                                                start=True, stop=True), r=[b_xcT], w=[psCb])
                        hp = 512 // T
                        npc = 16 // hp
                        MTv = MT[:T, 0:16 * T].rearrange("p (h l) -> p h l", h=16)
                        for q in range(npc):
                            ps, psb = self.psg[2 + q % 2]
                            df, dfb = dif[q % 2]
                            h0 = q * hp
                            Dm, b_Dm = Dms[q % 2]
                            Dv = Dm[:T, 0:hp * T].rearrange("p (h l) -> p h l", h=hp)
                            self.vop(lambda T=T, Dv=Dv, h0=h0, hp=hp: nc.vector.tensor_tensor(out=Dv, in0=self.ident_f[:T, :T].unsqueeze(1).to_broadcast([T, hp, T]),
                                                                                          in1=cum[:T, h0:h0 + hp].unsqueeze(2).to_broadcast([T, hp, T]), op=ALU.mult),
                                     r=[self.b_idf, b_cum], w=[b_Dm])
                            self.pop(lambda ps=ps, T=T, hp=hp, Dm=Dm: nc.tensor.matmul(ps[:T, 0:hp * T], lhsT=self.ones_f[:T, :T], rhs=Dm[:T, 0:hp * T],
                                                                                         start=True, stop=True), r=[self.b_ones, b_Dm], w=[psb])
                            dfv = df[:T, 0:hp * T].rearrange("p (h l) -> p h l", h=hp)
                            psv = ps[:T, 0:hp * T].rearrange("p (h l) -> p h l", h=hp)
                            self.vop(lambda dfv=dfv, psv=psv, T=T, h0=h0, hp=hp: nc.vector.tensor_tensor(out=dfv, in0=psv, in1=cum[:T, h0:h0 + hp].unsqueeze(2).to_broadcast([T, hp, T]),
                                                                                                     op=ALU.subtract), r=[psb, b_cum], w=[dfb])
                            self.vop(lambda dfv=dfv, T=T, hp=hp: nc.vector.tensor_tensor(out=dfv, in0=dfv, in1=self.negm[:T, :T].unsqueeze(1).to_broadcast([T, hp, T]), op=ALU.add),
                                     r=[dfb, self.b_negm], w=[dfb])
                            self.aop(lambda df=df, T=T, hp=hp: nc.scalar.activation(out=df[:T, 0:hp * T], in_=df[:T, 0:hp * T], func=AF.Exp), r=[dfb], w=[dfb])
                            if hp <= 8:
                                gi = h0 // 8
                                cbv = psC[:T, gi * 128:gi * 128 + T].unsqueeze(1).to_broadcast([T, hp, T])
                                self.vop(lambda dfv=dfv, cbv=cbv: nc.vector.tensor_tensor(out=dfv, in0=dfv, in1=cbv, op=ALU.mult), r=[dfb, psCb], w=[dfb])
                            else:
                                for gi in range(2):
                                    cbv = psC[:T, gi * 128:gi * 128 + T].unsqueeze(1).to_broadcast([T, 8, T])
                                    self.vop(lambda dfv=dfv, cbv=cbv, gi=gi: nc.vector.tensor_tensor(out=dfv[:, gi * 8:(gi + 1) * 8, :], in0=dfv[:, gi * 8:(gi + 1) * 8, :], in1=cbv, op=ALU.mult),
                                             r=[dfb, psCb], w=[dfb])
                            self.vop(lambda dfv=dfv, T=T, h0=h0, hp=hp: nc.vector.tensor_tensor(out=MTv[:, h0:h0 + hp, :], in0=dfv, in1=dtt[:T, h0:h0 + hp].unsqueeze(2).to_broadcast([T, hp, T]),
                                                                                            op=ALU.mult), r=[dfb, b_dt], w=[b_MT])
                        psY = [self.psg[2], self.psg[3]]
                        for hh in range(16):
                            ps, psb = psY[hh // 8]
                            self.pop(lambda ps=ps, hh=hh, T=T, ti=ti: nc.tensor.matmul(ps[:T, (hh % 8) * 64:(hh % 8 + 1) * 64], lhsT=MT[:T, hh * T:(hh + 1) * T],
                                                                                     rhs=x_tm[:T, ti, hh * 64:(hh + 1) * 64], start=True, stop=True),
                                     r=[b_MT, b_xtm], w=[psb])
                        psO = [self.psg[4], self.psg[5]]
                        for gi in range(2):
                            ps, psb = psO[gi]
                            self.pop(lambda ps=ps, gi=gi, cs=cs, T=T: nc.tensor.matmul(ps[:T, :], lhsT=xcT[:, 10 + gi, cs], rhs=hTb[:, gi * 512:(gi + 1) * 512], start=True, stop=True),
                                     r=[b_xcT, b_hTb], w=[psb])
                        for gi in range(2):
                            ps, psb = psO[gi]
                            self.vop(lambda ps=ps, gi=gi, T=T: nc.vector.tensor_tensor(out=y1[:T, gi * 512:(gi + 1) * 512].rearrange("p (h k) -> p h k", h=8),
                                                                                   in0=ps[:T, :].rearrange("p (h k) -> p h k", h=8),
                                                                                   in1=ecum[:T, gi * 8:(gi + 1) * 8].unsqueeze(2).to_broadcast([T, 8, 64]), op=ALU.mult),
                                     r=[psb, b_ecum], w=[b_y1])
                            ps2, psb2 = psY[gi]
                            self.vop(lambda ps2=ps2, gi=gi, T=T: nc.vector.tensor_tensor(out=y1[:T, gi * 512:(gi + 1) * 512], in0=y1[:T, gi * 512:(gi + 1) * 512], in1=ps2[:T, :], op=ALU.add),
                                     r=[psb2, b_y1], w=[b_y1])
                        self.vop(lambda T=T, ti=ti: nc.vector.tensor_tensor(out=y2[:T, :].rearrange("p (h k) -> p h k", h=16), in0=x_tm[:T, ti, :].rearrange("p (h k) -> p h k", h=16),
                                                                          in1=rowE[:T, 2, :].unsqueeze(2).to_broadcast([T, 16, 64]), op=ALU.mult), r=[b_xtm, b_rowE], w=[b_y2])
                        self.vop(lambda T=T: nc.vector.tensor_tensor(out=y2[:T, :], in0=y2[:T, :], in1=y1[:T, :], op=ALU.add), r=[b_y1, b_y2], w=[b_y2])
                        self.ld(dram["ylocal"][t * 128:t * 128 + T, :], y2[:T, :], r=[b_y2], w=[dbuf["ylocal"]])
                        self.vop(lambda T=T, ti=ti: nc.vector.tensor_tensor(out=xw[:T, :].rearrange("p (h k) -> p h k", h=16), in0=x_tm[:T, ti, :].rearrange("p (h k) -> p h k", h=16),
                                                                          in1=wgt[:T, :].unsqueeze(2).to_broadcast([T, 16, 64]), op=ALU.mult), r=[b_xtm, b_wgt], w=[b_xw])
                        psS = [self.psg[0], self.psg[1]]
                        for gi in range(2):
                            ps, psb = psS[gi]
                            self.pop(lambda ps=ps, gi=gi, T=T, ti=ti: nc.tensor.matmul(ps[:, :], lhsT=B_tm[:T, ti, gi * 128:(gi + 1) * 128], rhs=xw[:T, gi * 512:(gi + 1) * 512],
                                                                                     start=True, stop=True), r=[b_Btm, b_xw], w=[psb])
                        self.vop(lambda: nc.vector.tensor_tensor(out=hT[:, :].rearrange("p (h k) -> p h k", h=16), in0=hT[:, :].rearrange("p (h k) -> p h k", h=16),
                                                                 in1=edec[:, :].unsqueeze(2).to_broadcast([128, 16, 64]), op=ALU.mult), r=[b_hT, b_edec], w=[b_hT])
                        for gi in range(2):
                            ps, psb = psS[gi]
                            self.vop(lambda ps=ps, gi=gi: nc.vector.tensor_tensor(out=hT[:, gi * 512:(gi + 1) * 512], in0=hT[:, gi * 512:(gi + 1) * 512], in1=ps[:, :], op=ALU.add),
                                     r=[psb, b_hT], w=[b_hT])
                        self.aop(lambda: nc.scalar.copy(out=hTb[:, :], in_=hT[:, :]), r=[b_hT], w=[b_hTb])
                        self.vop(lambda sq=sq: nc.vector.tensor_tensor(out=rcl[sq][0][:, :], in0=rcl[sq][0][:, :], in1=clb[:, :], op=ALU.add), r=[rcl[sq][1], b_clb], w=[rcl[sq][1]])
                        off += T
                    last_of_seq = (sq == "s") or (gidx_ == 3)
                    if last_of_seq:
                        self.vop(lambda sq=sq: nc.vector.tensor_copy(out=Lst[sq][0][:], in_=hT[:]), r=[b_hT], w=[Lst[sq][1]])
                        pss = [self.psg[2], self.psg[3], self.psg[4]]
                        for c in range(12):
                            ps, psb = pss[c // 4]
                            self.pop(lambda ps=ps, c=c, sq=sq: nc.tensor.transpose(out=ps[0:3, (c % 4) * 128:(c % 4 + 1) * 128], in_=rhalo[sq][0][:, c, :], identity=self.ident_f[:]),
                                     r=[rhalo[sq][1], self.b_idf], w=[psb])
                        si = 0 if sq == "p" else 1
                        for k in range(3):
                            ps, psb = pss[k]
                            dstt, dstb = (y1, b_y1) if k < 2 else (y2, b_y2)
                            kk = k % 2
                            self.aop(lambda ps=ps, kk=kk, dstt=dstt: nc.scalar.copy(out=dstt[0:3, kk * 512:(kk + 1) * 512], in_=ps[0:3, :]), r=[psb], w=[dstb])
                        self.ld(dram["conv_out"][e, si, :, 0:1024], y1[0:3, :], r=[b_y1], w=[dbuf["conv_out"]])
                        self.ld(dram["conv_out"][e, si, :, 1024:1536], y2[0:3, 0:512], r=[b_y2], w=[dbuf["conv_out"]])
                fw.barrier()

            with ExitStack() as Ph:
                stall, b_stall = self.phase_alloc(Ph, "stall", [128, 8, 1040])
                lw, b_lw = self.phase_alloc(Ph, "lw", [128, 8, 16])
                lt, b_lt = self.phase_alloc(Ph, "lt", [128, 8, 16])
                so, b_so = self.phase_alloc(Ph, "so", [128, D])
                self.ld(dram["st_src"][:, 0:1024], Lst["p"][0][:], r=[Lst["p"][1]], w=[dbuf["st_src"]])
                self.ld(dram["st_src"][:, 1024:1040], rcl["p"][0][:], r=[rcl["p"][1]], w=[dbuf["st_src"]])
                self.allgather("st_src", "st_all")
                self.ld(stall[:], dram["st_all"].rearrange("(r p) f -> p r f", p=128), r=[dbuf["st_all"]], w=[b_stall])
                cm = cmask[:, 0:64].rearrange("p (j i) -> p j i", j=8)
                for i in range(8):
                    src = stall[:, i, 1024:1040].unsqueeze(1).to_broadcast([128, 8, 16])
                    mk = cm[:, :, i].unsqueeze(2).to_broadcast([128, 8, 16])
                    if i == 0:
                        self.vop(lambda src=src, mk=mk: nc.vector.tensor_tensor(out=lw[:], in0=src, in1=mk, op=ALU.mult), r=[b_stall, b_cmask], w=[b_lw])
                    else:
                        self.vop(lambda src=src, mk=mk: nc.vector.tensor_tensor(out=lt[:], in0=src, in1=mk, op=ALU.mult), r=[b_stall, b_cmask], w=[b_lt])
                        self.vop(lambda: nc.vector.tensor_tensor(out=lw[:], in0=lw[:], in1=lt[:], op=ALU.add), r=[b_lw, b_lt], w=[b_lw])
                self.aop(lambda: nc.scalar.activation(out=lw[:], in_=lw[:], func=AF.Exp), r=[b_lw], w=[b_lw])
                self.vop(lambda: nc.vector.tensor_tensor(out=lw[:], in0=lw[:], in1=cmask[:, 64:72].unsqueeze(2).to_broadcast([128, 8, 16]), op=ALU.mult),
                         r=[b_lw, b_cmask], w=[b_lw])
                hp_, hpb = hin["p"]
                for j in range(8):
                    wj = lw[:, j, :].unsqueeze(2).to_broadcast([128, 16, 64])
                    Lj = stall[:, j, 0:1024].rearrange("p (h k) -> p h k", h=16)
                    if j == 0:
                        self.vop(lambda wj=wj, Lj=Lj: nc.vector.tensor_tensor(out=hp_[:].rearrange("p (h k) -> p h k", h=16), in0=Lj, in1=wj, op=ALU.mult),
                                 r=[b_stall, b_lw], w=[hpb])
                    else:
                        self.vop(lambda wj=wj, Lj=Lj: nc.vector.tensor_tensor(out=so[:].rearrange("p (h k) -> p h k", h=16), in0=Lj, in1=wj, op=ALU.mult),
                                 r=[b_stall, b_lw], w=[b_so])
                        self.vop(lambda: nc.vector.tensor_tensor(out=hp_[:], in0=hp_[:], in1=so[:], op=ALU.add), r=[hpb, b_so], w=[hpb])
                for si, sq in enumerate("ps"):
                    self.aop(lambda sq=sq: nc.scalar.copy(out=hinb[sq][0][:], in_=hin[sq][0][:]), r=[hin[sq][1]], w=[hinb[sq][1]])
                    self.aop(lambda sq=sq: nc.scalar.activation(out=lt[:, 0, :], in_=rcl[sq][0][:], func=AF.Exp), r=[rcl[sq][1]], w=[b_lt])
                    self.vop(lambda sq=sq: nc.vector.tensor_tensor(out=so[:].rearrange("p (h k) -> p h k", h=16), in0=hin[sq][0][:].rearrange("p (h k) -> p h k", h=16),
                                                                 in1=lt[:, 0, :].unsqueeze(2).to_broadcast([128, 16, 64]), op=ALU.mult), r=[hin[sq][1], b_lt], w=[b_so])
                    self.vop(lambda sq=sq: nc.vector.tensor_tensor(out=so[:], in0=so[:], in1=Lst[sq][0][:], op=ALU.add), r=[b_so, Lst[sq][1]], w=[b_so])
                    self.ld(dram["ssm_out"][e, si], so[:], r=[b_so], w=[dbuf["ssm_out"]])
                fw.barrier()

            with ExitStack() as Ph:
                W_z, b_Wz = self.phase_alloc(Ph, "W_z", [128, 8, 1024], BF16)
                W_o, b_Wo = self.phase_alloc(Ph, "W_o", [128, 16, 1024], BF16)
                hnT, b_hnT = self.phase_alloc(Ph, "hnT", [128, 8, 128], BF16)
                yl, b_yl = self.phase_alloc(Ph, "yl", [128, D])
                sz, b_sz = self.phase_alloc(Ph, "sz", [128, D])
                yc, b_yc = self.phase_alloc(Ph, "yc", [128, D])
                yn, b_yn = self.phase_alloc(Ph, "yn", [128, D], BF16)
                ynT, b_ynT = self.phase_alloc(Ph, "ynT", [128, 8, 128], BF16)
                ypt, b_ypt = self.phase_alloc(Ph, "ypt", [128, 8, 128], BF16)
                rs2, b_rs2 = self.phase_alloc(Ph, "rs2", [128, 4])
                jk, b_jk = self.phase_alloc(Ph, "jk", [128, 512], BF16)
                tmp = self.norm_tmp(Ph, "p2")
                self.ldc(W_z[:], wie[:, :, 0:1024], r=[b_wie], w=[b_Wz])
                self.ldc(W_o[:], dram["w_out_e"][e].rearrange("(c p) n -> p c n", p=128), r=[dbuf["w_out_e"]], w=[b_Wo])
                for t in range(NT):
                    T = tsz(t)
                    sq = "p" if t < 16 else "s"
                    tk = slice(t * 128, t * 128 + T)
                    self.norm_T(t, 0, l, hnT[:, :, :T], b_hnT, tmp)
                    self.ld(yl[:T, :], dram["ylocal"][tk, :], r=[dbuf["ylocal"]], w=[b_yl])
                    self.ld(ypt[:, :, :T], dram["ypT_d"][:, :, tk], r=[dbuf["ypT_d"]], w=[b_ypt])
                    for half in range(2):
                        ps, psb = self.psg[half]
                        self.mm_acc(ps[:T, :], psb, [(hnT[:, kc, :T], W_z[:, kc, half * 512:(half + 1) * 512]) for kc in range(8)], [b_hnT, b_Wz])
                        self.aop(lambda ps=ps, half=half, T=T: nc.scalar.activation(out=sz[:T, half * 512:(half + 1) * 512], in_=ps[:T, :], func=AF.Silu), r=[psb], w=[b_sz])
                    for gi in range(2):
                        ps, psb = self.psg[2 + gi]
                        self.pop(lambda ps=ps, gi=gi, tk=tk, T=T, sq=sq: nc.tensor.matmul(ps[:T, :], lhsT=CT_all[:, gi, tk], rhs=hinb[sq][0][:, gi * 512:(gi + 1) * 512], start=True, stop=True),
                                 r=[b_CT, hinb[sq][1]], w=[psb])
                        self.vop(lambda ps=ps, gi=gi, T=T, t=t: nc.vector.tensor_tensor(out=yc[:T, gi * 512:(gi + 1) * 512].rearrange("p (h k) -> p h k", h=8),
                                                                                    in0=ps[:T, :].rearrange("p (h k) -> p h k", h=8),
                                                                                    in1=ect[:T, t, gi * 8:(gi + 1) * 8].unsqueeze(2).to_broadcast([T, 8, 64]), op=ALU.mult),
                                 r=[psb, b_ect], w=[b_yc])
                    self.vop(lambda T=T: nc.vector.tensor_tensor(out=yc[:T, :], in0=yc[:T, :], in1=yl[:T, :], op=ALU.add), r=[b_yc, b_yl], w=[b_yc])
                    self.vop(lambda T=T: nc.vector.tensor_tensor(out=yc[:T, :], in0=yc[:T, :], in1=sz[:T, :], op=ALU.mult), r=[b_yc, b_sz], w=[b_yc])
                    for gi in range(2):
                        self.rms_stats(yc[:T, gi * 512:(gi + 1) * 512], b_yc, T, rs2, b_rs2, jk, b_jk, n=512, col=gi)
                        self.vop(lambda gi=gi, T=T: nc.vector.scalar_tensor_tensor(out=yn[:T, gi * 512:(gi + 1) * 512], in0=yc[:T, gi * 512:(gi + 1) * 512], scalar=rs2[:T, gi:gi + 1],
                                                                               in1=ssdw[:T, gi * 512:(gi + 1) * 512], op0=ALU.mult, op1=ALU.mult), r=[b_yc, b_rs2, b_ssdw], w=[b_yn])
                    pt, ptb = self.next_pst()
                    for c in range(8):
                        self.pop(lambda c=c, pt=pt, T=T: nc.tensor.transpose(out=pt[:, c, :T], in_=yn[:T, c * 128:(c + 1) * 128], identity=self.ident_b[:T, :T]),
                                 r=[b_yn, self.b_idb], w=[ptb])
                    self.aop(lambda pt=pt, T=T: nc.scalar.copy(out=ynT[:, :, :T], in_=pt[:, :, :T]), r=[ptb], w=[b_ynT])
                    for half in range(2):
                        ps, psb = self.psg[4 + half]
                        pairs = [(ynT[:, c, :T], W_o[:, c, half * 512:(half + 1) * 512]) for c in range(8)] + \
                                [(ypt[:, c, :T], W_o[:, 8 + c, half * 512:(half + 1) * 512]) for c in range(8)]
                        self.mm_acc(ps[:T, :], psb, pairs, [b_ynT, b_ypt, b_Wo])
                        self.vop(lambda ps=ps, half=half, T=T, t=t: nc.vector.tensor_tensor(out=self.h[:T, t, half * 512:(half + 1) * 512], in0=self.h[:T, t, half * 512:(half + 1) * 512],
                                                                                        in1=ps[:T, :], op=ALU.add), r=[psb, self.hb[t]], w=[self.hb[t]])
                fw.barrier()

    def odd_layer(self, o, l):
        from contextlib import ExitStack
        nc, fw = self.nc, self.fw
        dram, dbuf = self.dram, self.dbuf
        allr = [list(range(NCORES))]
        fw.barrier()
        with ExitStack() as Ph:
            W_in, b_Win = self.phase_alloc(Ph, "W_in", [128, 8, 576], BF16)
            qkn, b_qkn = self.phase_alloc(Ph, "qkn", [128, 2, 256])
            ropet, b_ropet = self.phase_alloc(Ph, "ropet", [128, NT, 2, 32])
            hnT, b_hnT = self.phase_alloc(Ph, "hnT", [128, 8, 128], BF16)
            rs, b_rs = self.phase_alloc(Ph, "rsA", [128, 4])
            jk, b_jk = self.phase_alloc(Ph, "jkA", [128, 256], BF16)
            pay_tm, b_ptm = self.phase_alloc(Ph, "pay_tm", [128, 544], BF16)
            ckv32, b_c32 = self.phase_alloc(Ph, "ckv32", [128, 256])
            k32, b_k32 = self.phase_alloc(Ph, "k32", [128, 3, 32])
            payT, b_payT = self.phase_alloc(Ph, "payT", [128, 5, 128], BF16)
            tmp = self.norm_tmp(Ph, "pA")
            self.ldc(W_in[:], dram["w_in_o"][o].rearrange("(kc p) n -> p kc n", p=128), r=[dbuf["w_in_o"]], w=[b_Win])
            self.ld(qkn[:], dram["qkv_norm"][o:o + 1].to_broadcast([128, 2, 256]), w=[b_qkn])
            self.ld(ropet[:, 0:16], dram["rope_t"][0:2048].rearrange("(t p) a k -> p t a k", p=128), w=[b_ropet])
            self.ld(ropet[0:32, 16], dram["rope_t"][2048:2080], w=[b_ropet])
            for t in range(NT):
                T = tsz(t)
                tk = slice(t * 128, t * 128 + T)
                self.norm_T(t, 0, l, hnT[:, :, :T], b_hnT, tmp)
                ps0, psb0 = self.psg[(2 * t) % 6]
                ps1, psb1 = self.psg[(2 * t + 1) % 6]
                self.mm_acc(ps0[:T, 0:512], psb0, [(hnT[:, kc, :T], W_in[:, kc, 0:512]) for kc in range(8)], [b_hnT, b_Win])
                self.mm_acc(ps1[:T, 0:64], psb1, [(hnT[:, kc, :T], W_in[:, kc, 512:576]) for kc in range(8)], [b_hnT, b_Win])
                self.rms_stats(ps0[:T, 0:256], psb0, T, rs, b_rs, jk, b_jk, n=256, col=0)
                self.rms_stats(ps0[:T, 256:512], psb0, T, rs, b_rs, jk, b_jk, n=256, col=1)
                self.vop(lambda ps0=ps0, T=T: nc.vector.scalar_tensor_tensor(out=pay_tm[:T, 0:256], in0=ps0[:T, 0:256], scalar=rs[:T, 0:1], in1=qkn[:T, 0, :],
                                                                         op0=ALU.mult, op1=ALU.mult), r=[psb0, b_rs, b_qkn], w=[b_ptm])
                self.vop(lambda ps0=ps0, T=T: nc.vector.scalar_tensor_tensor(out=ckv32[:T, :], in0=ps0[:T, 256:512], scalar=rs[:T, 1:2], in1=qkn[:T, 1, :],
                                                                         op0=ALU.mult, op1=ALU.mult), r=[psb0, b_rs, b_qkn], w=[b_c32])
                self.ld(dram["ckv_out"][o, tk, :], ckv32[:T, :], r=[b_c32], w=[dbuf["ckv_out"]])
                self.aop(lambda T=T: nc.scalar.copy(out=pay_tm[:T, 256:512], in_=ckv32[:T, :]), r=[b_c32], w=[b_ptm])
                self.vop(lambda ps1=ps1, T=T, t=t: nc.vector.tensor_tensor(out=k32[:T, 0, :], in0=ps1[:T, 0:32], in1=ropet[:T, t, 0, :], op=ALU.mult), r=[psb1, b_ropet], w=[b_k32])
                self.vop(lambda ps1=ps1, T=T, t=t: nc.vector.tensor_tensor(out=k32[:T, 1, :], in0=ps1[:T, 32:64], in1=ropet[:T, t, 1, :], op=ALU.mult), r=[psb1, b_ropet], w=[b_k32])
                self.vop(lambda T=T: nc.vector.tensor_tensor(out=k32[:T, 2, :], in0=k32[:T, 0, :], in1=k32[:T, 1, :], op=ALU.add), r=[b_k32], w=[b_k32])
                self.ld(dram["kpe_out"][o, tk, :], k32[:T, 2, :], r=[b_k32], w=[dbuf["kpe_out"]])
                self.aop(lambda T=T: nc.scalar.copy(out=pay_tm[:T, 512:544], in_=k32[:T, 2, :]), r=[b_k32], w=[b_ptm])
                pt, ptb = self.next_pst()
                for c in range(4):
                    self.pop(lambda c=c, pt=pt, T=T: nc.tensor.transpose(out=pt[:, c, :T], in_=pay_tm[:T, c * 128:(c + 1) * 128], identity=self.ident_b[:T, :T]),
                             r=[b_ptm, self.b_idb], w=[ptb])
                self.pop(lambda pt=pt, T=T: nc.tensor.transpose(out=pt[0:32, 4, :T], in_=pay_tm[:T, 512:544], identity=self.ident_b[:T, :T]),
                         r=[b_ptm, self.b_idb], w=[ptb])
                self.aop(lambda pt=pt, T=T: nc.scalar.copy(out=payT[:, 0:4, :T], in_=pt[:, 0:4, :T]), r=[ptb], w=[b_payT])
                self.aop(lambda pt=pt, T=T: nc.scalar.copy(out=payT[0:32, 4, :T], in_=pt[0:32, 4, :T]), r=[ptb], w=[b_payT])
                self.ld(dram["pay"][0:512, :].rearrange("(c p) n -> p c n", p=128)[:, :, tk], payT[:, 0:4, :T], r=[b_payT], w=[dbuf["pay"]])
                self.ld(dram["pay"][512:544, tk], payT[0:32, 4, :T], r=[b_payT], w=[dbuf["pay"]])
            self.allgather("pay", "payall")
            fw.barrier()

        payall, b_pa = dram["payall"], dbuf["payall"]
        with ExitStack() as Ph:
            wq, b_wq = self.phase_alloc(Ph, "wq", [128, 2, 2, 128], BF16)
            wkv, b_wkv = self.phase_alloc(Ph, "wkv", [128, 2, 2, 128], BF16)
            KT, b_KT = self.phase_alloc(Ph, "KT", [128, 16384], BF16)
            Vt, b_V = self.phase_alloc(Ph, "Vt", [128, 128, 65], BF16)
            ckvT = [self.phase_alloc(Ph, f"ckvT{i}", [128, 2, 512], BF16) for i in range(2)]
            cqT = [self.phase_alloc(Ph, f"cqT{i}", [128, 2, 512], BF16) for i in range(2)]
            ropef = [self.phase_alloc(Ph, f"ropef{i}", [32, 2, 512]) for i in range(2)]
            QT = [self.phase_alloc(Ph, f"QT{i}", [128, 512], BF16) for i in range(2)]
            PTs = [self.phase_alloc(Ph, f"PT{i}", [128, 512], BF16) for i in range(3)]
            rt = [self.phase_alloc(Ph, f"rt{i}", [32, 512]) for i in range(2)]
            o_sb = [self.phase_alloc(Ph, f"o_sb{i}", [128, 4, 64], BF16) for i in range(2)]
            rec, b_rec = self.phase_alloc(Ph, "rec", [128, 8])
            ckvTc, b_cTc = self.phase_alloc(Ph, "ckvTc", [128, 2, 2048], BF16)
            ckvTn, b_cTn = self.phase_alloc(Ph, "ckvTn", [128, 2, 32], BF16)
            KTs, b_KTs = self.phase_alloc(Ph, "KTs", [128, 2080], BF16)
            Vs, b_Vs = self.phase_alloc(Ph, "Vs", [128, 17, 65], BF16)
            self.ldc(wq[:], dram["wq_c"][o].rearrange("(kc p) h d -> p kc h d", p=128), r=[dbuf["wq_c"]], w=[b_wq])
            self.ldc(wkv[:], dram["wkv_c"][o].rearrange("(kc p) h d -> p kc h d", p=128), r=[dbuf["wkv_c"]], w=[b_wkv])
            self.vop(lambda: nc.vector.memset(Vt[:, :, 64:65], 1.0), w=[b_V])
            self.vop(lambda: nc.vector.memset(Vs[:, :, 64:65], 1.0), w=[b_Vs])
            cnt = 0

            def build_q(cq_ap, rope_ap, N, hh, k):
                cq_t, cq_b = cqT[k]
                rp_t, rp_b = ropef[k]
                qt, qtb = QT[k]
                r1, r1b = rt[0]
                r2, r2b = rt[1]
                self.ld(cq_t[:, :, 0:N], cq_ap, r=[b_pa], w=[cq_b])
                self.ld(rp_t[:, :, 0:N], rope_ap, w=[rp_b])
                psQ, psQb = self.psg[4]
                psP, psPb = self.psg[5]
                self.mm_acc(psQ[0:64, 0:N], psQb, [(wq[:, kc, hh, 0:64], cq_t[:, kc, 0:N]) for kc in range(2)], [b_wq, cq_b])
                self.aop(lambda: nc.scalar.copy(out=qt[0:64, 0:N], in_=psQ[0:64, 0:N]), r=[psQb], w=[qtb])
                self.mm_acc(psP[0:32, 0:N], psPb, [(wq[:, kc, hh, 64:96], cq_t[:, kc, 0:N]) for kc in range(2)], [b_wq, cq_b])
                self.vop(lambda: nc.vector.tensor_tensor(out=r1[:, 0:N], in0=psP[0:32, 0:N], in1=rp_t[:, 0, 0:N], op=ALU.mult), r=[psPb, rp_b], w=[r1b])
                self.mm_acc(psQ[0:32, 0:N], psQb, [(wq[:, kc, hh, 96:128], cq_t[:, kc, 0:N]) for kc in range(2)], [b_wq, cq_b])
                self.vop(lambda: nc.vector.tensor_tensor(out=r2[:, 0:N], in0=psQ[0:32, 0:N], in1=rp_t[:, 1, 0:N], op=ALU.mult), r=[psQb, rp_b], w=[r2b])
                self.vop(lambda: nc.vector.tensor_tensor(out=qt[64:96, 0:N], in0=r1[:, 0:N], in1=r2[:, 0:N], op=ALU.add), r=[r1b, r2b], w=[qtb])
                return qt, qtb

            for hh in range(2):
                for r in range(8):
                    for g4 in range(4):
                        ck, ckb = ckvT[cnt % 2]
                        cnt += 1
                        kb = r * 2048 + g4 * 512
                        cols = slice(g4 * 512, (g4 + 1) * 512)
                        self.ld(ck[:], payall[r * 544 + 256:r * 544 + 512, :].rearrange("(kc p) n -> p kc n", p=128)[:, :, cols], r=[b_pa], w=[ckb])
                        self.ld(KT[64:96, kb:kb + 512], payall[r * 544 + 512:r * 544 + 544, cols], r=[b_pa], w=[b_KT])
                        ps, psb = self.psg[cnt % 2]
                        self.mm_acc(ps[0:64, :], psb, [(wkv[:, kc, hh, 0:64], ck[:, kc, :]) for kc in range(2)], [b_wkv, ckb])
                        self.aop(lambda ps=ps, kb=kb: nc.scalar.copy(out=KT[0:64, kb:kb + 512], in_=ps[0:64, :]), r=[psb], w=[b_KT])
                        ps2, psb2 = self.psg[2 + cnt % 2]
                        for j in range(4):
                            self.mm_acc(ps2[:, j * 64:(j + 1) * 64], psb2, [(ck[:, kc, j * 128:(j + 1) * 128], wkv[:, kc, hh, 64:128]) for kc in range(2)], [b_wkv, ckb])
                        kt0 = kb // 128
                        self.vop(lambda ps2=ps2, kt0=kt0: nc.vector.tensor_copy(out=Vt[:, kt0:kt0 + 4, 0:64], in_=ps2[:, 0:256].rearrange("p (j d) -> p j d", j=4)),
                                 r=[psb2], w=[b_V])
                for qg in range(32):
                    r, g4 = qg // 4, qg % 4
                    gbase = r * NTOK + g4 * 512
                    cols = slice(g4 * 512, (g4 + 1) * 512)
                    qt, qtb = build_q(payall[r * 544:r * 544 + 256, :].rearrange("(kc p) n -> p kc n", p=128)[:, :, cols],
                                      dram["rope_f"][:, :, gbase:gbase + 512].rearrange("a k n -> k a n"), 512, hh, qg % 2)
                    n_kt = 4 * qg + 4

                    def issue_S(kt):
                        j0 = max(0, kt - 4 * qg)
                        c0 = j0 * 128
                        psS, psSb = self.psg[4 + kt % 2]
                        pT, pTb = PTs[kt % 3]
                        self.pop(lambda: nc.tensor.matmul(psS[:, c0:512], lhsT=KT[0:96, kt * 128:(kt + 1) * 128], rhs=qt[0:96, c0:512], start=True, stop=True),
                                 r=[b_KT, qtb], w=[psSb])
                        self.aop(lambda: nc.scalar.activation(out=pT[:, c0:512], in_=psS[:, c0:512], func=AF.Exp, scale=SCALE), r=[psSb], w=[pTb])
                        if kt >= 4 * qg:
                            self.gop(lambda: nc.gpsimd.memset(pT[64:128, c0:c0 + 64], 0.0), w=[pTb])

                    def issue_PV(kt):
                        j0 = max(0, kt - 4 * qg)
                        pT, pTb = PTs[kt % 3]
                        for j in range(j0, 4):
                            pso, psob = self.psg[j]
                            self.pop(lambda j=j, pso=pso: nc.tensor.matmul(pso[:, 0:65], lhsT=pT[:, j * 128:(j + 1) * 128], rhs=Vt[:, kt, :],
                                                                       start=(kt == 0), stop=(kt == 4 * qg + j)), r=[pTb, b_V], w=[psob])

                    issue_S(0)
                    for kt in range(n_kt):
                        if kt + 1 < n_kt:
                            issue_S(kt + 1)
                        issue_PV(kt)
                    osb, osbb = o_sb[qg % 2]
                    for j in range(4):
                        pso, psob = self.psg[j]
                        self.vop(lambda pso=pso, j=j: nc.vector.reciprocal(out=rec[:, j:j + 1], in_=pso[:, 64:65]), r=[psob], w=[b_rec])
                        self.vop(lambda pso=pso, j=j, osb=osb: nc.vector.tensor_scalar(out=osb[:, j, :], in0=pso[:, 0:64], scalar1=rec[:, j:j + 1], scalar2=None, op0=ALU.mult),
                                 r=[psob, b_rec], w=[osbb])
                    self.ld(dram["o_d"][gbase:gbase + 512, hh * 64:(hh + 1) * 64].rearrange("(j p) d -> p j d", p=128), osb[:], r=[osbb], w=[dbuf["o_d"]])
            for b in range(8):
                self.ldc(ckvTc[:], dram["cache_ckvT"][o, b].rearrange("(kc p) n -> p kc n", p=128), r=[dbuf["cache_ckvT"]], w=[b_cTc])
                self.ld(ckvTn[:], payall[b * 544 + 256:b * 544 + 512, :].rearrange("(kc p) n -> p kc n", p=128)[:, :, 2048:2080], r=[b_pa], w=[b_cTn])
                self.ldc(KTs[64:96, 0:2048], dram["cache_kpeT"][o, b], r=[dbuf["cache_kpeT"]], w=[b_KTs])
                self.ld(KTs[64:96, 2048:2080], payall[b * 544 + 512:b * 544 + 544, 2048:2080], r=[b_pa], w=[b_KTs])
                for hh in range(2):
                    for g in range(4):
                        ps, psb = self.psg[g % 2]
                        self.mm_acc(ps[0:64, :], psb, [(wkv[:, kc, hh, 0:64], ckvTc[:, kc, g * 512:(g + 1) * 512]) for kc in range(2)], [b_wkv, b_cTc])
                        self.aop(lambda ps=ps, g=g: nc.scalar.copy(out=KTs[0:64, g * 512:(g + 1) * 512], in_=ps[0:64, :]), r=[psb], w=[b_KTs])
                        ps2, psb2 = self.psg[2 + g % 2]
                        for j in range(4):
                            kt = g * 4 + j
                            self.mm_acc(ps2[:, j * 64:(j + 1) * 64], psb2, [(ckvTc[:, kc, kt * 128:(kt + 1) * 128], wkv[:, kc, hh, 64:128]) for kc in range(2)], [b_wkv, b_cTc])
                        self.vop(lambda ps2=ps2, g=g: nc.vector.tensor_copy(out=Vs[:, g * 4:g * 4 + 4, 0:64], in_=ps2[:, 0:256].rearrange("p (j d) -> p j d", j=4)), r=[psb2], w=[b_Vs])
                    ps, psb = self.psg[0]
                    self.mm_acc(ps[0:64, 0:32], psb, [(wkv[:, kc, hh, 0:64], ckvTn[:, kc, :]) for kc in range(2)], [b_wkv, b_cTn])
                    self.aop(lambda ps=ps: nc.scalar.copy(out=KTs[0:64, 2048:2080], in_=ps[0:64, 0:32]), r=[psb], w=[b_KTs])
                    ps2, psb2 = self.psg[1]
                    self.mm_acc(ps2[0:32, 0:64], psb2, [(ckvTn[:, kc, :], wkv[:, kc, hh, 64:128]) for kc in range(2)], [b_wkv, b_cTn])
                    self.vop(lambda ps2=ps2: nc.vector.tensor_copy(out=Vs[0:32, 16, 0:64], in_=ps2[0:32, 0:64]), r=[psb2], w=[b_Vs])
                    gs = b * NTOK + 2048
                    qt, qtb = build_q(payall[b * 544:b * 544 + 256, :].rearrange("(kc p) n -> p kc n", p=128)[:, :, 2048:2080],
                                      dram["rope_f"][:, :, gs:gs + 32].rearrange("a k n -> k a n"), 32, hh, (b * 2 + hh) % 2)
                    pso, psob = self.psg[0]
                    for kt in range(17):
                        Kk = 128 if kt < 16 else 32
                        psS, psSb = self.psg[4 + kt % 2]
                        pT, pTb = PTs[kt % 3]
                        self.pop(lambda psS=psS, kt=kt, Kk=Kk, qt=qt: nc.tensor.matmul(psS[:Kk, 0:32], lhsT=KTs[0:96, kt * 128:kt * 128 + Kk], rhs=qt[0:96, 0:32], start=True, stop=True),
                                 r=[b_KTs, qtb], w=[psSb])
                        self.aop(lambda psS=psS, pT=pT, Kk=Kk: nc.scalar.activation(out=pT[:Kk, 0:32], in_=psS[:Kk, 0:32], func=AF.Exp, scale=SCALE), r=[psSb], w=[pTb])
                        self.pop(lambda pso=pso, pT=pT, kt=kt, Kk=Kk: nc.tensor.matmul(pso[0:32, 0:65], lhsT=pT[:Kk, 0:32], rhs=Vs[:Kk, kt, :], start=(kt == 0), stop=(kt == 16)),
                                 r=[pTb, b_Vs], w=[psob])
                    osb, osbb = o_sb[(b * 2 + hh) % 2]
                    self.vop(lambda pso=pso: nc.vector.reciprocal(out=rec[0:32, 4:5], in_=pso[0:32, 64:65]), r=[psob], w=[b_rec])
                    self.vop(lambda pso=pso, osb=osb: nc.vector.tensor_scalar(out=osb[0:32, 0, :], in0=pso[0:32, 0:64], scalar1=rec[0:32, 4:5], scalar2=None, op0=ALU.mult),
                             r=[psob, b_rec], w=[osbb])
                    self.ld(dram["o_d"][gs:gs + 32, hh * 64:(hh + 1) * 64], osb[0:32, 0, :], r=[osbb], w=[dbuf["o_d"]])
            self.allgather("o_d", "oall")
            fw.barrier()

        with ExitStack() as Ph:
            W_oo, b_Woo = self.phase_alloc(Ph, "W_oo", [128, 8, 1024], BF16)
            gix, b_gix = self.phase_alloc(Ph, "gix", [128, NT * 8], I32)
            ofull = [self.phase_alloc(Ph, f"ofull{i}", [128, 8, 128], BF16) for i in range(2)]
            oT, b_oT = self.phase_alloc(Ph, "oT", [128, 8, 128], BF16)
            self.ldc(W_oo[:], dram["w_out_o"][o].rearrange("(c p) n -> p c n", p=128), r=[dbuf["w_out_o"]], w=[b_Woo])
            self.ld(gix[:], dram["gidx"], w=[b_gix])
            for t in range(NT):
                T = tsz(t)
                of, ofb = ofull[t % 2]
                for r in range(8):
                    col = t * 8 + r
                    fw.dma(fw.qPOOL, lambda of=of, r=r, col=col, T=T: nc.gpsimd.indirect_dma_start(
                        out=of[:T, r, :], out_offset=None, in_=dram["oall"],
                        in_offset=bass.IndirectOffsetOnAxis(ap=gix[:T, col:col + 1], axis=0)), [dbuf["oall"], b_gix], [ofb])
                pt, ptb = self.next_pst()
                for r in range(8):
                    self.pop(lambda r=r, pt=pt, of=of, T=T: nc.tensor.transpose(out=pt[:, r, :T], in_=of[:T, r, :], identity=self.ident_b[:T, :T]),
                             r=[ofb, self.b_idb], w=[ptb])
                self.aop(lambda pt=pt, T=T: nc.scalar.copy(out=oT[:, :, :T], in_=pt[:, :, :T]), r=[ptb], w=[b_oT])
                for half in range(2):
                    ps, psb = self.psg[(2 * t + half) % 6]
                    self.mm_acc(ps[:T, :], psb, [(oT[:, r, :T], W_oo[:, r, half * 512:(half + 1) * 512]) for r in range(8)], [b_oT, b_Woo])
                    self.vop(lambda ps=ps, half=half, T=T, t=t: nc.vector.tensor_tensor(out=self.h[:T, t, half * 512:(half + 1) * 512], in0=self.h[:T, t, half * 512:(half + 1) * 512],
                                                                                    in1=ps[:T, :], op=ALU.add), r=[psb, self.hb[t]], w=[self.hb[t]])
            fw.barrier()

    def top16(self, src_ap, T, m_ap, ix_ap, work, work_b, src_b, m_b, ix_b):
        nc = self.nc
        n = src_ap.shape[-1]
        self.vop(lambda: nc.vector.max(out=m_ap[:, 0:8], in_=src_ap), r=[src_b], w=[m_b])
        self.vop(lambda: nc.vector.max_index(out=ix_ap[:, 0:8], in_max=m_ap[:, 0:8], in_values=src_ap), r=[src_b, m_b], w=[ix_b])
        self.vop(lambda: nc.vector.match_replace(out=work[:T, 0:n], in_to_replace=m_ap[:, 0:8], in_values=src_ap, imm_value=-1e30), r=[src_b, m_b], w=[work_b])
        self.vop(lambda: nc.vector.max(out=m_ap[:, 8:16], in_=work[:T, 0:n]), r=[work_b], w=[m_b])
        self.vop(lambda: nc.vector.max_index(out=ix_ap[:, 8:16], in_max=m_ap[:, 8:16], in_values=work[:T, 0:n]), r=[work_b, m_b], w=[ix_b])

    def peer_layer(self, l):
        from contextlib import ExitStack
        nc, fw = self.nc, self.fw
        dram, dbuf = self.dram, self.dbuf
        if self.dbg.get("_nopeer"):
            return
        fw.barrier()
        with ExitStack() as L:
            idx_all, b_idx = self.phase_alloc(L, "idx_all", [128, NT, 128], I32)
            g_all, b_g = self.phase_alloc(L, "g_all", [128, NT, 128])
            with ExitStack() as Ph:
                wq, b_wq = self.phase_alloc(Ph, "pwq", [128, 8, 2048], BF16)
                kT, b_kT = self.phase_alloc(Ph, "pkT", [128, 16, 128], BF16)
                xnT, b_xnT = self.phase_alloc(Ph, "pxnT", [128, 8, 128], BF16)
                qT, b_qT = self.phase_alloc(Ph, "pqT", [128, 16, 128], BF16)
                sc, b_sc = self.phase_alloc(Ph, "psc", [128, 2048])
                work, b_work = self.phase_alloc(Ph, "pwork", [128, 256])
                m, b_m = self.phase_alloc(Ph, "pm", [128, 16, 16])
                ix, b_ix = self.phase_alloc(Ph, "pix", [128, 16, 16], U32)
                ixf, b_ixf = self.phase_alloc(Ph, "pixf", [128, 16, 16])
                cand, b_cand = self.phase_alloc(Ph, "pcand", [128, 8, 256])
                tv, b_tv = self.phase_alloc(Ph, "ptv", [128, 8, 16])
                tp, b_tp = self.phase_alloc(Ph, "ptp", [128, 8, 16], U32)
                fi, b_fi = self.phase_alloc(Ph, "pfi", [128, 2, 8, 16], U32)
                fif, b_fif = self.phase_alloc(Ph, "pfif", [128, 2, 8, 16])
                eq, b_eq = self.phase_alloc(Ph, "peq", [128, 8, 16, 16])
                sel, b_sel = self.phase_alloc(Ph, "psel", [128, 2, 8, 16])
                ef, b_ef = self.phase_alloc(Ph, "pef", [128, 8, 16])
                sm, b_sm = self.phase_alloc(Ph, "psm", [128, 8])
                tmp = self.norm_tmp(Ph, "pp1")
                self.ldc(wq[:], dram["peer_wq"][l].rearrange("(kc p) n -> p kc n", p=128), r=[dbuf["peer_wq"]], w=[b_wq])
                self.ldc(kT[:], dram["keysT"][l].rearrange("s d k -> d s k"), r=[dbuf["keysT"]], w=[b_kT])
                for t in range(NT):
                    T = tsz(t)
                    self.norm_T(t, 1, l, xnT[:, :, :T], b_xnT, tmp)
                    for sg in range(16):
                        ps, psb = self.psg[sg % 4]
                        self.mm_acc(ps[:, 0:T], psb, [(wq[:, kc, sg * 128:(sg + 1) * 128], xnT[:, kc, :T]) for kc in range(8)], [b_wq, b_xnT])
                        self.aop(lambda ps=ps, sg=sg, T=T: nc.scalar.copy(out=qT[:, sg, :T], in_=ps[:, 0:T]), r=[psb], w=[b_qT])
                    for q4 in range(4):
                        ps, psb = self.psg[4 + q4 % 2]
                        for s4 in range(4):
                            sg = q4 * 4 + s4
                            self.pop(lambda ps=ps, sg=sg, s4=s4, T=T: nc.tensor.matmul(ps[:T, s4 * 128:(s4 + 1) * 128], lhsT=qT[:, sg, :T], rhs=kT[:, sg, :], start=True, stop=True),
                                     r=[b_qT, b_kT], w=[psb])
                        self.aop(lambda ps=ps, q4=q4, T=T: nc.scalar.copy(out=sc[:T, q4 * 512:(q4 + 1) * 512], in_=ps[:T, :]), r=[psb], w=[b_sc])
                    for sg in range(16):
                        self.top16(sc[:T, sg * 128:(sg + 1) * 128], T, m[:T, sg, :], ix[:T, sg, :], work, b_work, b_sc, b_m, b_ix)
                    mv = m[:T].rearrange("p (h c) k -> p h c k", c=2)
                    self.vop(lambda mv=mv, T=T: nc.vector.tensor_tensor(out=cand[:T].rearrange("p h (i j) -> p h i j", i=16),
                                                                    in0=mv[:, :, 0, :].unsqueeze(3).to_broadcast([T, 8, 16, 16]),
                                                                    in1=mv[:, :, 1, :].unsqueeze(2).to_broadcast([T, 8, 16, 16]), op=ALU.add), r=[b_m], w=[b_cand])
                    for hd in range(8):
                        self.top16(cand[:T, hd, :], T, tv[:T, hd, :], tp[:T, hd, :], work, b_work, b_cand, b_tv, b_tp)
                    self.vop(lambda T=T: nc.vector.tensor_single_scalar(out=fi[:T, 0], in_=tp[:T], scalar=4, op=ALU.logical_shift_right), r=[b_tp], w=[b_fi])
                    self.vop(lambda T=T: nc.vector.tensor_single_scalar(out=fi[:T, 1], in_=tp[:T], scalar=15, op=ALU.bitwise_and), r=[b_tp], w=[b_fi])
                    self.vop(lambda T=T: nc.vector.tensor_copy(out=fif[:T], in_=fi[:T]), r=[b_fi], w=[b_fif])
                    self.vop(lambda T=T: nc.vector.tensor_copy(out=ixf[:T], in_=ix[:T]), r=[b_ix], w=[b_ixf])
                    ixv = ixf[:T].rearrange("p (h c) k -> p h c k", c=2)
                    for c in range(2):
                        self.vop(lambda c=c, T=T: nc.vector.tensor_tensor(out=eq[:T], in0=self.iota16[:T, :].unsqueeze(1).unsqueeze(1).to_broadcast([T, 8, 16, 16]),
                                                                      in1=fif[:T, c].unsqueeze(3).to_broadcast([T, 8, 16, 16]), op=ALU.is_equal), r=[self.b_iota, b_fif], w=[b_eq])
                        self.vop(lambda c=c, T=T, ixv=ixv: nc.vector.tensor_tensor(out=eq[:T], in0=eq[:T], in1=ixv[:, :, c, :].unsqueeze(2).to_broadcast([T, 8, 16, 16]), op=ALU.mult),
                                 r=[b_eq, b_ixf], w=[b_eq])
                        self.vop(lambda c=c, T=T: nc.vector.tensor_reduce(out=sel[:T, c], in_=eq[:T], axis=AX.X, op=ALU.add), r=[b_eq], w=[b_sel])
                    self.vop(lambda T=T: nc.vector.tensor_scalar(out=ef[:T], in0=sel[:T, 0], scalar1=128.0, scalar2=float(0 if self.dbg.get("_smallpeer") else l * 16384), op0=ALU.mult, op1=ALU.add), r=[b_sel], w=[b_ef])
                    self.vop(lambda T=T: nc.vector.tensor_tensor(out=ef[:T], in0=ef[:T], in1=sel[:T, 1], op=ALU.add), r=[b_ef, b_sel], w=[b_ef])
                    self.vop(lambda T=T, t=t: nc.vector.tensor_copy(out=idx_all[:T, t, :], in_=ef[:T].rearrange("p h k -> p (h k)")), r=[b_ef], w=[b_idx])
                    self.vop(lambda T=T: nc.vector.tensor_tensor(out=ef[:T], in0=tv[:T], in1=tv[:T, :, 0:1].to_broadcast([T, 8, 16]), op=ALU.subtract), r=[b_tv, b_ef], w=[b_ef])
                    self.aop(lambda T=T: nc.scalar.activation(out=ef[:T], in_=ef[:T], func=AF.Exp), r=[b_ef], w=[b_ef])
                    self.vop(lambda T=T: nc.vector.tensor_reduce(out=sm[:T], in_=ef[:T], axis=AX.X, op=ALU.add), r=[b_ef], w=[b_sm])
                    self.vop(lambda T=T: nc.vector.reciprocal(out=sm[:T], in_=sm[:T]), r=[b_sm], w=[b_sm])
                    self.vop(lambda T=T, t=t: nc.vector.tensor_tensor(out=g_all[:T, t, :].rearrange("p (h k) -> p h k", h=8), in0=ef[:T],
                                                                  in1=sm[:T].unsqueeze(2).to_broadcast([T, 8, 16]), op=ALU.mult), r=[b_ef, b_sm], w=[b_g])
                fw.barrier()
            with ExitStack() as Ph:
                nrow, b_nrow = self.phase_alloc(Ph, "nrow", [128, D])
                xn, b_xn = self.phase_alloc(Ph, "pxn", [128, D], BF16)
                junk, b_junk = self.phase_alloc(Ph, "pjunk", [128, D], BF16)
                rs, b_rs = self.phase_alloc(Ph, "prs", [128, 4])
                act, b_act = self.phase_alloc(Ph, "pact", [128, 128])
                tg = [self.phase_alloc(Ph, f"ptg{i}", [128, 3, 4]) for i in range(2)]
                diag = [self.phase_alloc(Ph, f"pdiag{i}", [128, 4, 128], BF16) for i in range(3)]
                NB = 5
                Gb = []
                for i in range(NB):
                    gt_, _ = self.phase_alloc(Ph, f"G{i}", [128, 4, 2 * D], BF16)
                    Gb.append((gt_, [fw.buf() for _ in range(4)]))
                lyr = 0 if self.dbg.get("_smallpeer") else l
                cvt = self.cvt_b[lyr]
                self.ld(nrow[:], dram["norm_ffn"][l:l + 1, :].to_broadcast([128, D]), w=[b_nrow])
                gk = 0
                for t in range(NT):
                    T = tsz(t)
                    self.rms_stats(self.h[:T, t, :], self.hb[t], T, rs, b_rs, junk, b_junk)
                    self.vop(lambda: nc.vector.scalar_tensor_tensor(out=xn[:T, :], in0=self.h[:T, t, :], scalar=rs[:T, 0:1], in1=nrow[:T, :], op0=ALU.mult, op1=ALU.mult),
                             r=[self.hb[t], b_rs, b_nrow], w=[b_xn])
                    psA, psAb = self.psg[2 * (t % 2)]
                    psB, psBb = self.psg[2 * (t % 2) + 1]
                    grpG = {}

                    def gather(grp):
                        nonlocal gk
                        G, Gbuf = Gb[gk % NB]
                        gk += 1
                        grpG[grp] = (G, Gbuf)
                        for s_ in range(4):
                            slot = grp * 4 + s_
                            fw.dma(fw.qPOOL, lambda: nc.gpsimd.indirect_dma_start(
                                out=G[:T, s_, :], out_offset=None, in_=dram["peer_uvb"],
                                in_offset=bass.IndirectOffsetOnAxis(ap=idx_all[:T, t, slot:slot + 1], axis=0)), [cvt, b_idx], [Gbuf[s_]])

                    def udot(grp):
                        G, Gbuf = grpG[grp]
                        for s_ in range(4):
                            slot = grp * 4 + s_
                            self.vop(lambda: nc.vector.scalar_tensor_tensor(out=junk[:T, :], in0=G[:T, s_, 0:D], scalar=1.0, in1=xn[:T, :],
                                                                          op0=ALU.mult, op1=ALU.mult, accum_out=act[:T, slot:slot + 1]),
                                     r=[Gbuf[s_], b_xn], w=[b_junk, b_act])

                    def gelu1(grp):
                        tq, tqb = tg[grp % 2]
                        a = act[:T, grp * 4:grp * 4 + 4]
                        self.vop(lambda: nc.vector.tensor_tensor(out=tq[:T, 0, :], in0=a, in1=a, op=ALU.mult), r=[b_act], w=[tqb])
                        self.vop(lambda: nc.vector.tensor_scalar(out=tq[:T, 0, :], in0=tq[:T, 0, :], scalar1=0.044715, scalar2=1.0, op0=ALU.mult, op1=ALU.add), r=[tqb], w=[tqb])
                        self.vop(lambda: nc.vector.tensor_tensor(out=tq[:T, 0, :], in0=tq[:T, 0, :], in1=a, op=ALU.mult), r=[tqb, b_act], w=[tqb])
                        self.aop(lambda: nc.scalar.activation(out=tq[:T, 1, :], in_=tq[:T, 0, :], func=AF.Tanh, scale=0.7978845608028654), r=[tqb], w=[tqb])

                    def gelu2(grp):
                        tq, tqb = tg[grp % 2]
                        dg, dgb = diag[grp % 3]
                        a = act[:T, grp * 4:grp * 4 + 4]
                        self.vop(lambda: nc.vector.tensor_scalar(out=tq[:T, 1, :], in0=tq[:T, 1, :], scalar1=1.0, scalar2=0.5, op0=ALU.add, op1=ALU.mult), r=[tqb], w=[tqb])
                        self.vop(lambda: nc.vector.tensor_tensor(out=tq[:T, 1, :], in0=tq[:T, 1, :], in1=a, op=ALU.mult), r=[tqb, b_act], w=[tqb])
                        self.vop(lambda: nc.vector.tensor_tensor(out=tq[:T, 2, :], in0=tq[:T, 1, :], in1=g_all[:T, t, grp * 4:grp * 4 + 4], op=ALU.mult), r=[tqb, b_g], w=[tqb])
                        self.vop(lambda: nc.vector.tensor_tensor(out=dg[:T, :, :T], in0=self.ident_f[:T, :T].unsqueeze(1).to_broadcast([T, 4, T]),
                                                                 in1=tq[:T, 2, :].unsqueeze(2).to_broadcast([T, 4, T]), op=ALU.mult), r=[self.b_idf, tqb], w=[dgb])

                    def vsum(grp):
                        G, Gbuf = grpG[grp]
                        dg, dgb = diag[grp % 3]
                        for s_ in range(4):
                            slot = grp * 4 + s_
                            self.pop(lambda: nc.tensor.matmul(psA[:T, :], lhsT=dg[:T, s_, :T], rhs=G[:T, s_, D:D + 512], start=(slot == 0), stop=(slot == 127)),
                                     r=[Gbuf[s_], dgb], w=[psAb])
                            self.pop(lambda: nc.tensor.matmul(psB[:T, :], lhsT=dg[:T, s_, :T], rhs=G[:T, s_, D + 512:2 * D], start=(slot == 0), stop=(slot == 127)),
                                     r=[Gbuf[s_], dgb], w=[psBb])

                    for grp in range(32):
                        gather(grp)
                        udot(grp)
                        if grp >= 1:
                            gelu2(grp - 1)
                            vsum(grp - 1)
                        gelu1(grp)
                    gelu2(31)
                    vsum(31)
                    self.vop(lambda: nc.vector.tensor_tensor(out=self.h[:T, t, 0:512], in0=self.h[:T, t, 0:512], in1=psA[:T, :], op=ALU.add), r=[psAb, self.hb[t]], w=[self.hb[t]])
                    self.vop(lambda: nc.vector.tensor_tensor(out=self.h[:T, t, 512:1024], in0=self.h[:T, t, 512:1024], in1=psB[:T, :], op=ALU.add), r=[psBb, self.hb[t]], w=[self.hb[t]])
                fw.barrier()

    def ple_layer(self, l):
        from contextlib import ExitStack
        nc, fw = self.nc, self.fw
        dram, dbuf = self.dram, self.dbuf
        fw.barrier()
        with ExitStack() as Ph:
            W_g, b_Wg = self.phase_alloc(Ph, "W_g", [128, 8, 1024], BF16)
            W_p, b_Wp = self.phase_alloc(Ph, "W_p", [128, 2, 1024], BF16)
            hnT, b_hnT = self.phase_alloc(Ph, "hnT", [128, 8, 128], BF16)
            pTt = [self.phase_alloc(Ph, f"pTt{i}", [128, 2, 128], BF16) for i in range(2)]
            sg = [self.phase_alloc(Ph, f"sg{i}", [128, 512]) for i in range(2)]
            tmp = self.norm_tmp(Ph, "ple")
            self.ldc(W_g[:], dram["w_ple_gate"][l].rearrange("(kc p) n -> p kc n", p=128), r=[dbuf["w_ple_gate"]], w=[b_Wg])
            self.ldc(W_p[:], dram["w_ple_proj"][l].rearrange("(kc p) n -> p kc n", p=128), r=[dbuf["w_ple_proj"]], w=[b_Wp])
            for t in range(NT):
                T = tsz(t)
                tk = slice(t * 128, t * 128 + T)
                self.norm_T(t, 2, l, hnT[:, :, :T], b_hnT, tmp)
                pt_, ptb_ = pTt[t % 2]
                self.ldc(pt_[:, :, :T], dram["pT"][l].rearrange("(kc p) n -> p kc n", p=128)[:, :, tk], r=[dbuf["pT"]], w=[ptb_])
                for half in range(2):
                    hs = slice(half * 512, (half + 1) * 512)
                    psG, psGb = self.psg[2 * half]
                    psP, psPb = self.psg[2 * half + 1]
                    s_, sb_ = sg[half]
                    self.mm_acc(psG[:T, :], psGb, [(hnT[:, kc, :T], W_g[:, kc, hs]) for kc in range(8)], [b_hnT, b_Wg])
                    self.aop(lambda psG=psG, s_=s_, T=T: nc.scalar.activation(out=s_[:T, :], in_=psG[:T, :], func=AF.Sigmoid), r=[psGb], w=[sb_])
                    self.mm_acc(psP[:T, :], psPb, [(pt_[:, kc, :T], W_p[:, kc, hs]) for kc in range(2)], [ptb_, b_Wp])
                    self.vop(lambda psP=psP, s_=s_, T=T: nc.vector.tensor_tensor(out=s_[:T, :], in0=s_[:T, :], in1=psP[:T, :], op=ALU.mult), r=[psPb, sb_], w=[sb_])
                    self.vop(lambda s_=s_, T=T, t=t, hs=hs: nc.vector.tensor_tensor(out=self.h[:T, t, hs], in0=self.h[:T, t, hs], in1=s_[:T, :], op=ALU.add),
                             r=[sb_, self.hb[t]], w=[self.hb[t]])
            fw.barrier()


def _rope_tables(pos):
    half = 16
    inv = (10000.0 ** (-np.arange(half, dtype=np.float32) / half)).astype(np.float32)
    ang = pos.astype(np.float32)[:, None] * inv[None, :]
    cos, sin = np.cos(ang).astype(np.float32), np.sin(ang).astype(np.float32)
    cs = np.concatenate([cos, cos], axis=1)
    sn = np.concatenate([-sin, sin], axis=1)
    return cs, sn


def prep_inputs(inp):
    f = lambda a: np.ascontiguousarray(a, dtype=np.float32)
    rot = np.concatenate([np.arange(16, 32), np.arange(0, 16)])
    shared = {}
    ncols = np.stack([inp["norm_mix"], inp["norm_ffn"], inp["ple_norm"]], 0)
    shared["ncols"] = f(ncols.reshape(3, 4, 8, 128).transpose(3, 0, 1, 2))
    shared["final_norm"] = f(inp["final_norm"].reshape(1, D))
    shared["norm_ffn"] = f(inp["norm_ffn"])
    shared["w_in_e"] = f(inp["w_in_e"])
    colE = np.zeros((2, 128, 68), np.float32)
    for e in range(2):
        colE[e, :, 0:48] = inp["conv_w"][e].T.reshape(12, 128, 4).transpose(1, 0, 2).reshape(128, 48)
        colE[e, :, 48:60] = inp["conv_b"][e].reshape(12, 128).T
        colE[e, :, 60:68] = inp["pool_scale"][e].reshape(8, 128).T
    shared["colE"] = colE
    shared["rowE"] = f(np.stack([inp["dt_bias"], inp["a_log"], inp["d_skip"]], 1))
    shared["ssd_norm_w"] = f(inp["ssd_norm_w"])
    shared["pool_w"] = f(inp["pool_w"])
    shared["w_out_e"] = f(inp["w_out_e"])
    wio = inp["w_in_o"]
    shared["w_in_o"] = f(np.concatenate([wio, wio[:, :, 512 + rot]], axis=2))
    shared["qkv_norm"] = f(np.stack([inp["q_norm"], inp["kv_norm"]], 1))
    shared["w_out_o"] = f(inp["w_out_o"])
    shared["peer_wq"] = f(inp["peer_wq"])
    shared["keysT"] = f(inp["peer_keys"].reshape(4, 16, 128, 128).transpose(0, 1, 3, 2))
    shared["peer_uv"] = np.concatenate([inp["peer_u"].reshape(4 * 16384, D), inp["peer_v"].reshape(4 * 16384, D)], axis=1).astype(np.float32)
    shared["w_ple_proj"] = f(inp["w_ple_proj"])
    shared["w_ple_gate"] = f(inp["w_ple_gate"])
    shared["cache_ckvT"] = f(inp["cache_ckv"].transpose(0, 1, 3, 2))
    shared["cache_kpeT"] = f(inp["cache_kpe"].transpose(0, 1, 3, 2))
    gpos = np.concatenate([np.concatenate([np.arange(2048 * c, 2048 * c + 2048), 2048 + np.arange(32)]) for c in range(8)])
    cs, sn = _rope_tables(gpos)
    shared["rope_f"] = f(np.stack([cs.T, sn.T], 0))
    wuq = inp["w_uq"].reshape(2, 256, 16, 96)
    wukv = inp["w_ukv"].reshape(2, 256, 16, 128)
    maps = []
    for c in range(NCORES):
        m = dict(shared)
        m["xs"] = f(np.concatenate([inp["x_prompt"][0, 2048 * c:2048 * c + 2048], inp["x_sample"][c]], 0))
        pc = np.concatenate([inp["p_prompt"][:, 0, 2048 * c:2048 * c + 2048], inp["p_sample"][:, c]], 1)
        m["pT"] = f(pc.transpose(0, 2, 1))
        hq = wuq[:, :, 2 * c:2 * c + 2, :]
        m["wq_c"] = f(np.concatenate([hq, hq[..., 64 + rot]], axis=-1))
        m["wkv_c"] = f(wukv[:, :, 2 * c:2 * c + 2, :])
        m["st_conv"] = f(inp["state_conv"][:, c].transpose(0, 2, 1).reshape(2, 12, 128, 3).transpose(0, 2, 1, 3))
        m["st_pool"] = f(inp["state_pool"][:, c].transpose(0, 2, 1).reshape(2, 8, 128, 15).transpose(0, 2, 1, 3))
        m["st_ssmT"] = f(inp["state_ssm"][:, c].reshape(2, 1024, 128).transpose(0, 2, 1))
        lpos = np.concatenate([np.arange(2048 * c, 2048 * c + 2048), 2048 + np.arange(32)])
        cs_t, sn_t = _rope_tables(lpos)
        m["rope_t"] = f(np.stack([cs_t, sn_t], 1))
        pc_ = np.ones((4, 15), np.float32)
        if c == 0:
            for gi, w in enumerate((2, 4, 8, 16)):
                pc_[gi] = w / np.minimum(np.arange(15) + 1, w)
        m["pcorr"] = pc_.reshape(1, 60)
        cm = np.zeros((8, 8), np.float32)
        for j in range(8):
            for i in range(8):
                cm[j, i] = 1.0 if (j < i < c) else 0.0
        v = (np.arange(8) < c).astype(np.float32)
        oh = (np.arange(8) == c - 1).astype(np.float32)
        m["cmask"] = np.concatenate([cm.reshape(-1), v, oh]).reshape(1, 80).astype(np.float32)
        gi_ = np.zeros((128, NT * 8), np.int32)
        for t in range(NT):
            for r in range(8):
                gi_[:, t * 8 + r] = r * NG + c * NTOK + t * 128 + np.arange(128)
        gi_ = np.minimum(gi_, NCORES * NG - 1)
        m["gidx"] = gi_
        maps.append(m)
    return maps


_PROG = {}


def get_prog(dbg=None, nlayers=4):
    key = (tuple(sorted((dbg or {}).items())), nlayers)
    if key not in _PROG:
        _PROG[key] = Prog(dbg=dbg, nlayers=nlayers)
    return _PROG[key]


def run(inp, dbg=None, nlayers=4):
    prog = get_prog(dbg, nlayers)
    maps = prep_inputs(inp)
    if (dbg or {}).get("_nopeer"):
        for m in maps:
            m["peer_uv"] = m["peer_uv"][:128]
    res = run_bass_kernel_spmd(prog.nc, maps, core_ids=list(range(NCORES)))
    return res.results


def assemble(rs):
    y = np.stack([r["y"] for r in rs], 0)
    y_prompt = y[:, :2048].reshape(1, 16384, D)
    y_sample = y[:, 2048:]
    ckv = np.stack([r["ckv_out"] for r in rs], 0)
    kpe = np.stack([r["kpe_out"] for r in rs], 0)
    ckv_p = ckv[:, :, :2048].transpose(1, 0, 2, 3).reshape(2, 1, 16384, 256)
    kpe_p = kpe[:, :, :2048].transpose(1, 0, 2, 3).reshape(2, 1, 16384, 32)
    ckv_s = ckv[:, :, 2048:].transpose(1, 0, 2, 3)
    kpe_s = kpe[:, :, 2048:].transpose(1, 0, 2, 3)
    conv = np.stack([r["conv_out"] for r in rs], 0)
    pool = np.stack([r["pool_out"] for r in rs], 0)
    ssm = np.stack([r["ssm_out"] for r in rs], 0)
    conv_p = conv[7, :, 0][:, None]
    pool_p = pool[7, :, 0][:, None]
    ssm_l = ssm.transpose(0, 1, 2, 4, 3).reshape(8, 2, 2, 16, 64, 128)
    ssm_p = ssm_l[7, :, 0][:, None]
    conv_s = conv[:, :, 1].transpose(1, 0, 2, 3)
    pool_s = pool[:, :, 1].transpose(1, 0, 2, 3)
    ssm_s = ssm_l[:, :, 1].transpose(1, 0, 2, 3, 4)
    c = np.ascontiguousarray
    return tuple(c(a, dtype=np.float32) for a in (y_prompt, y_sample, conv_p, ssm_p, pool_p, ckv_p, kpe_p,
                                                   conv_s, ssm_s, pool_s, ckv_s, kpe_s))


def kernel(**inputs):
    inp = {k: np.asarray(v) for k, v in inputs.items()}
    rs = run(inp)
    return assemble(rs)
```
